# Optimizing a Trainium2 kernel written in Bass

```python
import jax, jax.numpy as jnp
from jax import lax
import numpy as np


D_MODEL = 1024
BATCH = 8
SEQ = 4096
DEPTH = 4

HEAD_DIM = 64
N_HEADS = D_MODEL // HEAD_DIM
MIX_WIDTH = N_HEADS * HEAD_DIM
KV_HEADS_A = max(1, N_HEADS // 8)
KV_HEADS_B = max(1, N_HEADS // 4)
KV_HEADS_C = max(1, N_HEADS // 4)
ROT_DIM = HEAD_DIM // 4
ROPE_THETA = 500000.0
WINDOW_A = 128
DILATED_GROUPS = ((128, 1), (512, 4), (2048, 16))
BAND_BLOCK = 128
MOBA_BLOCK = 256
MOBA_TOPK = 3
MOBA_QCHUNK = 128
N_MIXERS = 3
DEEPNORM_ALPHA = (2 * DEPTH) ** 0.25
DEEPNORM_BETA = (8 * DEPTH) ** -0.25
LN_EPS = 1e-5
ATTN_SCALE = HEAD_DIM ** -0.5

kernel_name = 'hybrid_swa_dilated_moba_deepnorm'


def _in_layout(kind):
    q_cols = N_HEADS * HEAD_DIM
    if kind == 0:
        kv = KV_HEADS_A * HEAD_DIM
        return [(q_cols, False), (kv, False), (kv, True), (MIX_WIDTH, False)]
    if kind == 1:
        kv = KV_HEADS_B * HEAD_DIM
        parts = []
        for _ in DILATED_GROUPS:
            parts += [(q_cols, False), (kv, False), (kv, True)]
        return parts + [(MIX_WIDTH, False)]
    kv = KV_HEADS_C * HEAD_DIM
    return [(q_cols, False), (kv, False), (kv, True), (MIX_WIDTH, False)]


def _split(h, kind):
    sizes = [c for c, _ in _in_layout(kind)]
    offs = [int(o) for o in np.cumsum(sizes)[:-1]]
    return jnp.split(h, offs, axis=-1)


def _heads(t, n):
    return t.reshape(t.shape[0], t.shape[1], n, HEAD_DIM)


def _rope_tables(seq):
    inv = ROPE_THETA ** (-jnp.arange(0, ROT_DIM, 2, dtype=jnp.float32) / ROT_DIM)
    ang = jnp.arange(seq, dtype=jnp.float32)[:, None] * inv[None, :]
    return jnp.cos(ang), jnp.sin(ang)


def _partial_rope(t, cos, sin):
    half = ROT_DIM // 2
    tr = t[..., :ROT_DIM].astype(jnp.float32)
    t1, t2 = tr[..., :half], tr[..., half:]
    c = cos[None, :, None, :]
    s = sin[None, :, None, :]
    rot = jnp.concatenate([t1 * c - t2 * s, t2 * c + t1 * s], axis=-1).astype(t.dtype)
    return jnp.concatenate([rot, t[..., ROT_DIM:]], axis=-1)


def _layer_norm(t, g, b):
    tf = t.astype(jnp.float32)
    mu = tf.mean(-1, keepdims=True)
    var = jnp.square(tf - mu).mean(-1, keepdims=True)
    return ((tf - mu) * lax.rsqrt(var + LN_EPS) * g.astype(jnp.float32) + b.astype(jnp.float32)).astype(t.dtype)


def _banded_attention(q, k, v, max_dist, sink=None):
    nbat, L, H, hd = q.shape
    kvh = k.shape[2]
    g = H // kvh
    nb = L // BAND_BLOCK
    qb = q.reshape(nbat, nb, BAND_BLOCK, kvh, g, hd)

    def with_prev(t):
        tb = t.reshape(nbat, nb, BAND_BLOCK, kvh, hd)
        prev = jnp.pad(tb, ((0, 0), (1, 0), (0, 0), (0, 0), (0, 0)))[:, :-1]
        return jnp.concatenate([prev, tb], axis=2)

    kk, vv = with_prev(k), with_prev(v)
    s = jnp.einsum('bnqkgd,bnskd->bnkgqs', qb, kk).astype(jnp.float32) * ATTN_SCALE
    qpos = jnp.arange(nb)[:, None, None] * BAND_BLOCK + jnp.arange(BAND_BLOCK)[None, :, None]
    kpos = jnp.arange(nb)[:, None, None] * BAND_BLOCK - BAND_BLOCK + jnp.arange(2 * BAND_BLOCK)[None, None, :]
    dist = qpos - kpos
    mask = (dist >= 0) & (dist <= max_dist) & (kpos >= 0)
    s = jnp.where(mask[None, :, None, None], s, -jnp.inf)
    m = s.max(-1)
    if sink is not None:
        sk = sink.astype(jnp.float32).reshape(kvh, g)[None, None, :, :, None]
        m = jnp.maximum(m, sk)
    p = jnp.exp(s - m[..., None])
    denom = p.sum(-1)
    if sink is not None:
        denom = denom + jnp.exp(sk - m)
    o = jnp.einsum('bnkgqs,bnskd->bnqkgd', p.astype(v.dtype), vv).astype(jnp.float32)
    o = o / denom.transpose(0, 1, 4, 2, 3)[..., None]
    lse = (m + jnp.log(denom)).transpose(0, 1, 4, 2, 3)
    return o.reshape(nbat, L, H, hd), lse.reshape(nbat, L, H)


def _dilated_branch(q, k, v, window, dil):
    nbat, S = q.shape[0], q.shape[1]
    unit = dil * BAND_BLOCK
    s_pad = -(-S // unit) * unit
    L = s_pad // dil

    def fold(t):
        t = jnp.pad(t, ((0, 0), (0, s_pad - S), (0, 0), (0, 0)))
        nh = t.shape[2]
        t = t.reshape(nbat, L, dil, nh, HEAD_DIM).transpose(0, 2, 1, 3, 4)
        return t.reshape(nbat * dil, L, nh, HEAD_DIM)

    o, lse = _banded_attention(fold(q), fold(k), fold(v), window // dil)
    o = o.reshape(nbat, dil, L, N_HEADS, HEAD_DIM).transpose(0, 2, 1, 3, 4).reshape(nbat, s_pad, N_HEADS, HEAD_DIM)
    lse = lse.reshape(nbat, dil, L, N_HEADS).transpose(0, 2, 1, 3).reshape(nbat, s_pad, N_HEADS)
    return o[:, :S], lse[:, :S]


def _moba_attention(q, k, v):
    nbat, S, H, hd = q.shape
    kvh = k.shape[2]
    g = H // kvh
    s_pad = -(-S // MOBA_BLOCK) * MOBA_BLOCK
    padw = ((0, 0), (0, s_pad - S), (0, 0), (0, 0))
    q, k, v = jnp.pad(q, padw), jnp.pad(k, padw), jnp.pad(v, padw)
    nb = s_pad // MOBA_BLOCK
    kb = k.reshape(nbat, nb, MOBA_BLOCK, kvh, hd)
    vb = v.reshape(nbat, nb, MOBA_BLOCK, kvh, hd)
    kmean = kb.astype(jnp.float32).mean(axis=2)
    qg = q.reshape(nbat, s_pad, kvh, g, hd)
    gate = jnp.einsum('bskgd,bnkd->bskgn', qg.astype(jnp.float32), kmean)
    own = jnp.arange(s_pad) // MOBA_BLOCK
    past = jnp.arange(nb)[None, :] < own[:, None]
    gate = jnp.where(past[None, :, None, None, :], gate, -jnp.inf)
    n_top = min(MOBA_TOPK, nb)
    _, sel = lax.top_k(gate, n_top)
    valid = sel < own[None, :, None, None, None]
    nq = s_pad // MOBA_QCHUNK
    kbt = kb.transpose(0, 3, 1, 2, 4)
    vbt = vb.transpose(0, 3, 1, 2, 4)
    hidx = jnp.arange(kvh)[None, :, None, None]

    def chunk(args):
        qc, selc, validc, bi, ci = args
        kt, vt = kbt[bi], vbt[bi]
        ks = kt[hidx, selc]
        vs = vt[hidx, selc]
        ob = (ci * MOBA_QCHUNK) // MOBA_BLOCK
        ko = lax.dynamic_index_in_dim(kt, ob, axis=1, keepdims=False)
        vo = lax.dynamic_index_in_dim(vt, ob, axis=1, keepdims=False)
        s_sel = jnp.einsum('qkgd,qkgtmd->qkgtm', qc, ks).astype(jnp.float32) * ATTN_SCALE
        s_sel = jnp.where(validc[..., None], s_sel, -jnp.inf)
        s_own = jnp.einsum('qkgd,kmd->qkgm', qc, ko).astype(jnp.float32) * ATTN_SCALE
        qpos = ci * MOBA_QCHUNK + jnp.arange(MOBA_QCHUNK)
        kpos = ob * MOBA_BLOCK + jnp.arange(MOBA_BLOCK)
        s_own = jnp.where((kpos[None, :] <= qpos[:, None])[:, None, None, :], s_own, -jnp.inf)
        m = jnp.maximum(s_sel.max((-2, -1)), s_own.max(-1))
        p_sel = jnp.exp(s_sel - m[..., None, None])
        p_own = jnp.exp(s_own - m[..., None])
        denom = p_sel.sum((-2, -1)) + p_own.sum(-1)
        o = (jnp.einsum('qkgtm,qkgtmd->qkgd', p_sel.astype(v.dtype), vs).astype(jnp.float32)
             + jnp.einsum('qkgm,kmd->qkgd', p_own.astype(v.dtype), vo).astype(jnp.float32))
        return o / denom[..., None]

    xs = (qg.reshape(nbat * nq, MOBA_QCHUNK, kvh, g, hd),
          sel.reshape(nbat * nq, MOBA_QCHUNK, kvh, g, n_top),
          valid.reshape(nbat * nq, MOBA_QCHUNK, kvh, g, n_top),
          jnp.repeat(jnp.arange(nbat, dtype=jnp.int32), nq),
          jnp.tile(jnp.arange(nq, dtype=jnp.int32), nbat))
    out = lax.map(chunk, xs)
    return out.reshape(nbat, s_pad, H, hd)[:, :S]


def _mixer_a(h, sink, cos, sin):
    q, k, v, z = _split(h, 0)
    q = _partial_rope(_heads(q, N_HEADS), cos, sin)
    k = _partial_rope(_heads(k, KV_HEADS_A), cos, sin)
    o, _ = _banded_attention(q, k, _heads(v, KV_HEADS_A), WINDOW_A - 1, sink)
    return o.reshape(h.shape[0], h.shape[1], MIX_WIDTH).astype(h.dtype), z


def _mixer_b(h, cos, sin):
    parts = _split(h, 1)
    z = parts[-1]
    outs, lses = [], []
    for gi, (window, dil) in enumerate(DILATED_GROUPS):
        q, k, v = parts[3 * gi], parts[3 * gi + 1], parts[3 * gi + 2]
        q = _partial_rope(_heads(q, N_HEADS), cos, sin)
        k = _partial_rope(_heads(k, KV_HEADS_B), cos, sin)
        o, lse = _dilated_branch(q, k, _heads(v, KV_HEADS_B), window, dil)
        outs.append(o)
        lses.append(lse)
    w = jax.nn.softmax(jnp.stack(lses, axis=0), axis=0)
    o = (w[..., None] * jnp.stack(outs, axis=0)).sum(0)
    return o.reshape(h.shape[0], h.shape[1], MIX_WIDTH).astype(h.dtype), z


def _mixer_c(h, cos, sin):
    q, k, v, z = _split(h, 2)
    q = _partial_rope(_heads(q, N_HEADS), cos, sin)
    k = _partial_rope(_heads(k, KV_HEADS_C), cos, sin)
    o = _moba_attention(q, k, _heads(v, KV_HEADS_C))
    return o.reshape(h.shape[0], h.shape[1], MIX_WIDTH).astype(h.dtype), z


def _make_w_in(kit, kind):
    parts = []
    for cols, is_v in _in_layout(kind):
        scale = (D_MODEL ** -0.5) * (DEEPNORM_BETA if is_v else 1.0)
        parts.append(jax.random.normal(next(kit), (D_MODEL, cols), jnp.float32) * scale)
    return jnp.concatenate(parts, axis=1)


def setup_inputs(seed: int = 0) -> dict:
    key = jax.random.key(seed)
    kit = iter(jax.random.split(key, 64))
    out = {'x': jax.random.normal(next(kit), (BATCH, SEQ, D_MODEL), jnp.float32)}
    for i in range(DEPTH):
        kind = i % N_MIXERS
        out['w_in_%d' % i] = _make_w_in(kit, kind)
        if kind == 0:
            out['sink_%d' % i] = jax.random.normal(next(kit), (N_HEADS,), jnp.float32) * 0.5
        out['w_out_%d' % i] = jax.random.normal(next(kit), (MIX_WIDTH, D_MODEL), jnp.float32) * (MIX_WIDTH ** -0.5) * DEEPNORM_BETA
        out['ln_g_%d' % i] = 1.0 + 0.01 * jax.random.normal(next(kit), (D_MODEL,), jnp.float32)
        out['ln_b_%d' % i] = 0.01 * jax.random.normal(next(kit), (D_MODEL,), jnp.float32)
    return out


def reference(x, w_in_0, sink_0, w_out_0, ln_g_0, ln_b_0,
              w_in_1, w_out_1, ln_g_1, ln_b_1,
              w_in_2, w_out_2, ln_g_2, ln_b_2,
              w_in_3, sink_3, w_out_3, ln_g_3, ln_b_3):
    layers = [(w_in_0, sink_0, w_out_0, ln_g_0, ln_b_0),
              (w_in_1, None, w_out_1, ln_g_1, ln_b_1),
              (w_in_2, None, w_out_2, ln_g_2, ln_b_2),
              (w_in_3, sink_3, w_out_3, ln_g_3, ln_b_3)]
    cos, sin = _rope_tables(x.shape[1])
    for i in range(DEPTH):
        w_in, sink, w_out, ln_g, ln_b = layers[i]
        kind = i % N_MIXERS
        h = x @ w_in
        if kind == 0:
            mix, z = _mixer_a(h, sink, cos, sin)
        elif kind == 1:
            mix, z = _mixer_b(h, cos, sin)
        else:
            mix, z = _mixer_c(h, cos, sin)
        y = mix * jax.nn.silu(z)
        x = _layer_norm(DEEPNORM_ALPHA * x + y @ w_out, ln_g, ln_b)
    return x
```

```python
import numpy as np
import concourse.bass as bass
import concourse.mybir as mybir
from concourse.ap import AP
from concourse.bass_utils import run_bass_kernel_spmd

F32 = mybir.dt.float32
BF16 = mybir.dt.bfloat16
AF = mybir.ActivationFunctionType
ALU = mybir.AluOpType
AX = mybir.AxisListType

S = 4096
D = 1024
T = 512
NT = S // T
ALPHA = 8.0 ** 0.25
EPS = 1e-5
SCALE = 0.125
BIG = 30000.0
DIL = ((128, 1), (512, 4), (2048, 16))


class Prog:
    def __init__(self, nc):
        self.nc = nc
        self.engs = ["pe", "act", "dve", "pool", "sp"]
        self.streams = {e: [] for e in self.engs}
        self.count = {e: 0 for e in self.engs}
        self.semh = {}
        self.dma_n = {e: 0 for e in self.engs}
        self.NDS = 8
        self.waited = {e: {} for e in self.engs}
        self.lastw = {}
        self.readers = {}

    def sem(self, key):
        if key not in self.semh:
            self.semh[key] = self.nc.alloc_semaphore(key)
        return self.semh[key]

    def op(self, eng, fn, reads=(), writes=(), dma=False):
        writes = list(writes) + [r for r in reads if r.startswith("ps") and r not in writes]
        reads = [r for r in reads if not r.startswith("ps")]
        deps = {}

        def add(tok):
            if tok is None:
                return
            k, v, e = tok
            if e == eng and eng == "pe" and k == "c_pe":
                return
            if deps.get(k, 0) < v:
                deps[k] = v

        for r in reads:
            add(self.lastw.get(r))
        for w in writes:
            add(self.lastw.get(w))
            for tok in self.readers.get(w, {}).values():
                add(tok)
        if dma:
            i = self.dma_n[eng]
            self.dma_n[eng] += 1
            k = "d_%s_%d" % (eng, i % self.NDS)
            val = 16 * (i // self.NDS + 1)
            if val > 16:
                if deps.get(k, 0) < val - 16:
                    deps[k] = val - 16
            tok = (k, val, eng)
        else:
            self.count[eng] += 1
            tok = ("c_" + eng, self.count[eng], eng)
        waits = []
        for k, v in deps.items():
            if self.waited[eng].get(k, 0) < v:
                self.waited[eng][k] = v
                waits.append((k, v))
        self.streams[eng].append((waits, fn, tok[0], 16 if dma else 1))
        for r in reads:
            d = self.readers.setdefault(r, {})
            if tok[0] not in d or d[tok[0]][1] < tok[1]:
                d[tok[0]] = tok
        for w in writes:
            self.lastw[w] = tok
            self.readers[w] = {}
        return tok

    def mm(self, out, lhsT, rhs, start, stop, reads, writes):
        self.op("pe", lambda e: e.matmul(out, lhsT, rhs, start=start, stop=stop,
                                         skip_group_check=True), reads, writes)

    def tr(self, out, in_, ident, reads, writes):
        self.op("pe", lambda e: e.transpose(out, in_, ident), reads, writes)

    def act(self, out, in_, func, reads, writes, **kw):
        self.op("act", lambda e: e.activation(out, in_, func, **kw), reads, writes)

    def tt(self, eng, out, in0, in1, op, reads, writes):
        self.op(eng, lambda e: e.tensor_tensor(out, in0, in1, op), reads, writes)

    def ts(self, eng, out, in0, s1, s2, op0, op1, reads, writes):
        if s2 is None:
            self.op(eng, lambda e: e.tensor_scalar(out, in0, s1, None, op0), reads, writes)
        else:
            self.op(eng, lambda e: e.tensor_scalar(out, in0, s1, s2, op0, op1), reads, writes)

    def stt(self, eng, out, in0, scalar, in1, op0, op1, reads, writes):
        self.op(eng, lambda e: e.scalar_tensor_tensor(out, in0, scalar, in1, op0, op1),
                reads, writes)

    def cp(self, eng, out, in_, reads, writes):
        if eng == "act":
            self.op(eng, lambda e: e.activation(out, in_, AF.Copy), reads, writes)
        else:
            self.op(eng, lambda e: e.tensor_copy(out, in_), reads, writes)

    def recip(self, eng, out, in_, reads, writes):
        self.op(eng, lambda e: e.reciprocal(out, in_), reads, writes)

    def reduce_add(self, eng, out, in_, reads, writes):
        self.op(eng, lambda e: e.tensor_reduce(out, in_, AX.X, ALU.add), reads, writes)

    def max8(self, eng, out, in_, reads, writes):
        self.op(eng, lambda e: e.max(out, in_), reads, writes)

    def memset(self, eng, ap, val, writes):
        self.op(eng, lambda e: e.memset(ap, val), (), writes)

    def dma(self, q, out, in_, reads, writes):
        self.op(q, lambda e: e.dma_start(out=out, in_=in_), reads, writes, dma=True)

    def barrier(self):
        allres = list(self.lastw.keys() | self.readers.keys())
        for e in ["pe", "act", "dve", "pool", "sp"]:
            pass
        self._bar = getattr(self, "_bar", 0) + 1
        toks = {}
        for r in allres:
            t = self.lastw.get(r)
            if t is not None and toks.get(t[0], (0,))[0] < t[1]:
                toks[t[0]] = (t[1], t[2])
            for t in self.readers.get(r, {}).values():
                if toks.get(t[0], (0,))[0] < t[1]:
                    toks[t[0]] = (t[1], t[2])
        for e in self.engs:
            waits = []
            for k, (v, te) in toks.items():
                if te == e and e == "pe" and k == "c_pe":
                    continue
                if self.waited[e].get(k, 0) < v:
                    self.waited[e][k] = v
                    waits.append((k, v))
            if waits:
                self.streams[e].append((waits, None, None, 0))
        self.lastw = {}
        self.readers = {}

    def emit(self):
        nc = self.nc
        self.barrier()
        emap = {"pe": "tensor", "act": "scalar", "dve": "vector", "pool": "gpsimd", "sp": "sync"}
        with nc.Block() as block:
            for eng in self.engs:
                stream = self.streams[eng]

                def body(e, stream=stream):
                    for waits, fn, inc, amt in stream:
                        for k, v in waits:
                            e.wait_ge(self.sem(k), v)
                        if fn is not None:
                            fn(e).then_inc(self.sem(inc), amt)

                getattr(block, emap[eng])(body)


class Arena:
    def __init__(self, t, ncols):
        self.t = t
        self.n = ncols
        self.off = 0

    def reset(self):
        self.off = 0

    def alloc(self, cols, dtype=BF16):
        w = cols * (2 if dtype == F32 else 1)
        self.off = (self.off + 15) // 16 * 16
        a = self.off
        self.off += w
        assert self.off <= self.n, ("arena overflow", self.off, self.n)
        ap = self.t[:, a:a + w]
        if dtype == F32:
            ap = ap.bitcast(F32)
        return ap


def layer_cols(kind):
    if kind == 0:
        return [(0, 1024, 1152)], 1280, 2
    if kind == 1:
        return [(g * 1536, g * 1536 + 1024, g * 1536 + 1280) for g in range(3)], 4608, 4
    return [(0, 1024, 1280)], 1536, 4


def head_pairs(kind):
    if kind == 0:
        return [(h, h + 8, 0) for h in range(8)]
    return [(h, h + 4, 0) for h in range(4)] + [(h, h + 4, 1) for h in range(8, 12)]


def build_program(NL=4, dbg_mix=False, stop=None):
    import os
    stop = stop or os.environ.get('K_STOP')
    nc = bass.Bass("TRN2", target_bir_lowering=False)
    P = Prog(nc)
    kinds = [i % 3 for i in range(NL)]
    def din(name, shape):
        return nc.dram_tensor(name, shape, F32, kind="ExternalInput").ap()

    xT = din("xT", [D, S])
    outT = nc.dram_tensor("outT", [D, S], F32, kind="ExternalOutput").ap()
    w_in, w_out, lng, lnb, sink = [], [], [], [], []
    for i in range(NL):
        ncols = {0: 2304, 1: 5632, 2: 2560}[kinds[i]]
        w_in.append(din("w_in_%d" % i, [D, ncols]))
        w_out.append(din("w_out_%d" % i, [D, D]))
        lng.append(din("lng_%d" % i, [128, 8]))
        lnb.append(din("lnb_%d" % i, [128, 8]))
        sink.append(din("sink_%d" % i, [1, 16]) if kinds[i] == 0 else None)
    ropeC_d = din("ropeC", [128, S])
    ropeS_d = din("ropeS", [128, S])
    perm_d = din("permT", [128, 128])
    mswa_d = din("mask_swa", [128, 512])
    mdil_d = din("mask_dil", [128, 512])
    mown_d = din("mask_own", [128, 4 * 512])
    erow_d = din("erows", [16, S])
    ident_d = din("ident", [128, 128])

    def dscr(name, shape, dt=BF16):
        return nc.dram_tensor(name, shape, dt, kind="Internal").ap()

    qs = dscr("qs", [3, D, S])
    ks = dscr("ks", [3, 256, S])
    vs = dscr("vs", [3, S, 256])
    zs = dscr("zs", [D, S])
    ys = dscr("ys", [D, S])
    q32 = dscr("q32", [D, S], F32)
    ksum_d = dscr("ksum_d", [256, 16], F32)

    XH = nc.alloc_sbuf_tensor("XH", [128, 8, S], BF16)
    XL = nc.alloc_sbuf_tensor("XL", [128, 8, S], BF16)
    ACOLS = 40448
    arena_t = nc.alloc_sbuf_tensor("arena", [128, ACOLS], BF16)
    A = Arena(arena_t, ACOLS)
    ps = [nc.alloc_psum_tensor("ps%d" % i, [128, 512], F32)[:, :] for i in range(8)]

    def xres(j, t):
        return "X_%d_%d" % (j, t)

    A.reset()
    stg = [A.alloc(512, F32) for _ in range(3)]
    n = 0
    for t in range(NT):
        for j in range(8):
            b = n % 3
            n += 1
            P.dma("sp", stg[b], xT[128 * j:128 * j + 128, T * t:T * t + T], [], ["stg%d" % b])
            P.cp("act", XH[:, j, T * t:T * t + T], stg[b], ["stg%d" % b], [xres(j, t) + "h"])
            P.tt("dve", XL[:, j, T * t:T * t + T], stg[b], XH[:, j, T * t:T * t + T],
                 ALU.subtract, ["stg%d" % b, xres(j, t) + "h"], [xres(j, t) + "l"])
    P.barrier()
    if stop == 'load':
        return _finish(nc, P)

    for li in range(NL):
        kind = kinds[li]
        groups, zbase, nkv = layer_cols(kind)
        pairs = head_pairs(kind)
        nkvp = nkv // 2
        ng = len(groups)
        wv = w_in[li].rearrange("(kc p) n -> p kc n", p=128)

        _skip = os.environ.get('K_SKIP_PA') == '1'
        A.reset()
        ropeC = A.alloc(S, F32)
        ropeS = A.alloc(S, F32)
        Wb = [A.alloc(1024) for _ in range(3)]
        Wstg = [A.alloc(1024, F32) for _ in range(2)]
        permT = A.alloc(128)
        qb = [A.alloc(512) for _ in range(2)]
        t1 = [A.alloc(512, F32) for _ in range(2)]
        t2 = [A.alloc(512, F32) for _ in range(2)]
        qr = [A.alloc(512, F32) for _ in range(2)]
        stb = [A.alloc(512) for _ in range(3)]
        ksum = A.alloc(32, F32)
        P.dma("sp", ropeC, ropeC_d, [], ["ropeC"])
        P.dma("sp", ropeS, ropeS_d, [], ["ropeS"])
        P.dma("pool", permT, perm_d, [], ["permT"])

        chunks = []
        for g, (qbs, kbs, vbs) in enumerate(groups):
            for kp in range(nkvp):
                chunks.append(("K", g, kp, [(kbs + 128 * kp, 128, 0)]))
                chunks.append(("V", g, kp, [(vbs + 128 * kp, 128, 0)]))
        for pi, (ha, hb, kp) in enumerate(pairs):
            for g, (qbs, kbs, vbs) in enumerate(groups):
                chunks.append(("Q", g, pi, [(qbs + 64 * ha, 64, 0), (qbs + 64 * hb, 64, 64)]))
            chunks.append(("Z", 0, pi, [(zbase + 64 * ha, 64, 0), (zbase + 64 * hb, 64, 64)]))

        cnt = {"w": 0, "ps": 0, "pp": 0, "qb": 0, "t": 0, "st": 0}
        _kt = os.environ.get('K_TYPES')
        if _kt:
            chunks = [c for c in chunks if c[0] in _kt][:int(os.environ.get('K_NCH', '100'))]
        pending = []

        def flush_pending():
            while pending:
                pending.pop(0)()

        if _skip:
            chunks = []
        def issue_w(ci):
            wi_, si_ = ci % 3, ci % 2
            Wst = Wstg[si_].rearrange("p (kc n) -> p kc n", kc=8)
            for (c0, ncs, d0) in chunks[ci][3]:
                P.dma("sp", Wst[:, :, d0:d0 + ncs], wv[:, :, c0:c0 + ncs], [], ["Wst%d" % si_])
            P.cp("pool", Wb[wi_], Wstg[si_], ["Wst%d" % si_], ["W%d" % wi_])

        if chunks:
            issue_w(0)
        for ci, (typ, g, idx, srcs) in enumerate(chunks):
            wi = ci % 3
            W = Wb[wi].rearrange("p (kc n) -> p kc n", kc=8)
            if ci + 1 < len(chunks):
                issue_w(ci + 1)
            if typ == "V":
                for s4 in range(8):
                    pb = cnt["ps"] % 3
                    cnt["ps"] += 1
                    for ss in range(4):
                        s = 4 * s4 + ss
                        for kc in range(8):
                            P.mm(ps[pb][:, 128 * ss:128 * ss + 128], XH[:, kc, 128 * s:128 * s + 128],
                                 W[:, kc, :], kc == 0, kc == 7,
                                 ["W%d" % wi, xres(kc, s // 4) + "h"], ["ps%d" % pb])
                    sb = cnt["st"] % 3
                    cnt["st"] += 1
                    P.cp("act", stb[sb], ps[pb], ["ps%d" % pb], ["stb%d" % sb])
                    dst = vs[g].rearrange("(s p) c -> p s c", p=128)[:, 4 * s4:4 * s4 + 4, 128 * idx:128 * idx + 128]
                    P.dma("sp", dst, stb[sb].rearrange("p (s c) -> p s c", s=4), ["stb%d" % sb], ["vs"])
                continue
            for t in range(NT):
                pb = cnt["ps"] % 3
                cnt["ps"] += 1
                for kc in range(8):
                    P.mm(ps[pb], W[:, kc, :], XH[:, kc, T * t:T * t + T], kc == 0, kc == 7,
                         ["W%d" % wi, xres(kc, t) + "h"], ["ps%d" % pb])
                flush_pending()
                if typ == "Z":
                    sb = cnt["st"] % 3
                    cnt["st"] += 1
                    P.act(stb[sb], ps[pb], AF.Silu, ["ps%d" % pb], ["stb%d" % sb])
                    P.dma("sp", zs[128 * idx:128 * idx + 128, T * t:T * t + T], stb[sb],
                          ["stb%d" % sb], ["zs"])
                    continue
                qi = cnt["qb"] % 2
                cnt["qb"] += 1
                P.cp("act", qb[qi], ps[pb], ["ps%d" % pb], ["qb%d" % qi])
                _lvl = int(os.environ.get('K_LVL', '9'))
                if _lvl == 1:
                    P.dma("sp", ks[g, 128 * idx:128 * idx + 128, T * t:T * t + T], qb[qi], ["qb%d" % qi], ["ks"])
                    continue
                P.tt("dve", t1[qi], ps[pb], ropeC[:, T * t:T * t + T], ALU.mult,
                     ["ps%d" % pb, "ropeC"], ["t1_%d" % qi])
                if _lvl == 2:
                    continue
                if _lvl == 3:
                    pp = 3 + cnt["pp"] % 2
                    cnt["pp"] += 1
                    P.mm(ps[pp], permT, qb[qi], True, True, ["permT", "qb%d" % qi], ["ps%d" % pp])
                    continue

                def second(qi=qi, typ=typ, g=g, idx=idx, t=t):
                    pp = 3 + cnt["pp"] % 2
                    cnt["pp"] += 1
                    P.mm(ps[pp], permT, qb[qi], True, True, ["permT", "qb%d" % qi], ["ps%d" % pp])
                    P.tt("dve", t2[qi], ps[pp], ropeS[:, T * t:T * t + T], ALU.mult,
                         ["ps%d" % pp, "ropeS"], ["t2_%d" % qi])
                    sb = cnt["st"] % 3
                    cnt["st"] += 1
                    need32 = (kind == 2)
                    if need32:
                        P.tt("dve", qr[qi], t1[qi], t2[qi], ALU.add,
                             ["t1_%d" % qi, "t2_%d" % qi], ["qr%d" % qi])
                        P.cp("act", stb[sb], qr[qi], ["qr%d" % qi], ["stb%d" % sb])
                    else:
                        P.tt("dve", stb[sb], t1[qi], t2[qi], ALU.add,
                             ["t1_%d" % qi, "t2_%d" % qi], ["stb%d" % sb])
                    if typ == "Q":
                        P.dma("sp", qs[g, 128 * idx:128 * idx + 128, T * t:T * t + T], stb[sb],
                              ["stb%d" % sb], ["qs"])
                        if need32:
                            P.dma("sp", q32[128 * idx:128 * idx + 128, T * t:T * t + T], qr[qi],
                                  ["qr%d" % qi], ["q32"])
                    else:
                        P.dma("sp", ks[g, 128 * idx:128 * idx + 128, T * t:T * t + T], stb[sb],
                              ["stb%d" % sb], ["ks"])
                        if need32:
                            P.reduce_add("dve", ksum[:, 2 * t:2 * t + 2],
                                         qr[qi].rearrange("p (b k) -> p b k", b=2),
                                         ["qr%d" % qi], ["ksum"])

                pending.append(second)
            flush_pending()
            if typ == "K" and kind == 2:
                P.dma("sp", ksum_d[128 * idx:128 * idx + 128, :], ksum[:, 0:16], ["ksum"], ["ksum_d"])
        P.barrier()
        if stop and stop[0] == 'P' and li == NL - 1:
            which = stop[1:]
            src = {'q0': qs[0], 'q1': qs[1], 'q2': qs[2], 'z': zs}.get(which)
            if which == 'XH':
                for c in range(8):
                    P.dma("pool", outT[128 * c:128 * c + 128, :], XH[:, c, :], [], ["outT"])
            elif src is not None:
                for c in range(8):
                    P.dma("pool", outT[128 * c:128 * c + 128, :], src[128 * c:128 * c + 128, :], [], ["outT"])
            elif which[0] == 'k':
                g = int(which[1])
                for c in range(2):
                    P.dma("pool", outT[128 * c:128 * c + 128, :], ks[g, 128 * c:128 * c + 128, :], [], ["outT"])
            elif which[0] == 'v':
                g = int(which[1])
                for c in range(8):
                    P.dma("pool", outT[0:256, 512 * c:512 * c + 512].rearrange("c s -> s c"),
                          vs[g, 512 * c:512 * c + 512, :], [], ["outT"])
            return _finish(nc, P)

        A.reset()
        if os.environ.get('K_PAD'):
            A.alloc(int(os.environ['K_PAD']))
        yst = [A.alloc(512) for _ in range(2)]
        SZ = [A.alloc(512) for _ in range(3)]
        loads_list = []
        _tight = kind == 1 or os.environ.get('K_TIGHT') == '1'
        NPT = 3 if _tight else 4
        NQB = 2 if _tight else 3
        PT = [A.alloc(512) for _ in range(NPT)]
        rd = A.alloc(512, F32)
        gg = rd
        initrow = A.alloc(16 * 128 if kind == 0 else 128)
        onesrow = A.alloc(512)
        P.memset("dve", initrow[0:1, :], 0.0, ["initrow"])
        P.memset("dve", onesrow[0:1, :], 1.0, ["onesrow"])
        if kind == 0:
            sk = A.alloc(16, F32)
            P.dma("sp", sk[0:1, :], sink[li], [], ["sk"])
            for h in range(16):
                isB = h >= 8
                c0 = 128 * h + (0 if isB else 64)
                P.act(initrow[0:1, c0:c0 + 64], sk[0:1, h:h + 1].to_broadcast([1, 64]), AF.Exp,
                      ["sk", "initrow"], ["initrow"])

        VW = 32 * 192

        def load_v(g, kp, Vt, dil, tag):
            V3 = Vt.rearrange("p (b c) -> p b c", c=192)
            P.memset("pool", V3[:, :, 64:128], 1.0, ["Vt" + tag])
            nb = 32 // dil
            for r in range(dil):
                for half in range(2):
                    c0 = 128 * kp + 64 * half
                    src = vs[g].rearrange("(b p r) c -> r p b c", p=128, r=dil)[r][:, :, c0:c0 + 64]
                    dst = V3[:, nb * r:nb * (r + 1), 128 * half:128 * half + 64]
                    P.dma("sp", dst, src, ["vs"], ["Vt" + tag])

        def load_kv(g, kp, KT, Vt, dil, tag):
            P.dma("sp", KT, ks[g, 128 * kp:128 * kp + 128, :], ["ks"], ["KT" + tag])
            load_v(g, kp, Vt, dil, tag)

        def vaug(Vt, kt, isB):
            base = 192 * kt + (64 if isB else 0)
            return Vt[:, base:base + 128]

        items = []
        st = {"s": 0, "pt": 0, "o": 0, "ld": 0}

        def epilogue(pi, t, oA, oB, szb):
            P.recip("dve", rd[0:64, :], ps[oA][64:128, :], ["ps%d" % oA], ["gg"])
            P.recip("dve", rd[64:128, :], ps[oB][0:64, :], ["ps%d" % oB], ["gg"])
            P.tt("pool", gg, rd, SZ[szb], ALU.mult, ["SZ%d" % szb, "gg"], ["gg"])
            yb = st["ld"] % 2
            if os.environ.get('K_DEN') == '1':
                P.cp("dve", yst[yb][0:64, :], ps[oA][64:128, :], ["ps%d" % oA], ["yst%d" % yb])
                P.cp("dve", yst[yb][64:128, :], ps[oB][0:64, :], ["ps%d" % oB], ["yst%d" % yb])
                P.dma("sp", ys[128 * pi:128 * pi + 128, T * t:T * t + T], yst[yb], ["yst%d" % yb], ["ys"])
                st["ld"] += 1
                return
            P.tt("dve", yst[yb][0:64, :], ps[oA][0:64, :], gg[0:64, :], ALU.mult,
                 ["ps%d" % oA, "gg"], ["yst%d" % yb])
            P.tt("dve", yst[yb][64:128, :], ps[oB][64:128, :], gg[64:128, :], ALU.mult,
                 ["ps%d" % oB, "gg"], ["yst%d" % yb])
            P.dma("sp", ys[128 * pi:128 * pi + 128, T * t:T * t + T], yst[yb], ["yst%d" % yb], ["ys"])
            st["ld"] += 1

        if kind in (0, 1):
            mask4 = A.alloc(512)
            P.dma("pool", mask4, mswa_d if kind == 0 else mdil_d, [], ["mask4"])
            KTs = [A.alloc(S) for _ in range(ng)]
            Vts = [A.alloc(VW) for _ in range(ng)]
            QTs = [[A.alloc(512) for _ in range(NQB)] for _ in range(ng)]
            cur_kp = -1
            for pi, (ha, hb, kp) in enumerate(pairs):
                newkp = kp != cur_kp
                cur_kp = kp

                def kvload(kp=kp):
                    for g in range(ng):
                        if kind == 1 and str(g) not in os.environ.get('K_GRP', '012') + os.environ.get('K_LDG', ''):
                            continue
                        load_kv(g, kp, KTs[g], Vts[g], DIL[g][1] if kind == 1 else 1, str(g))
                for t in range(NT):
                    ti = pi * NT + t
                    lb = ti % 3
                    lq = ti % NQB
                    flush_before = newkp and t == 0

                    def loads(pi=pi, t=t, lb=lb, lq=lq):
                        for g in range(ng):
                            if kind == 1 and str(g) not in os.environ.get('K_GRP', '012') + os.environ.get('K_LDQ', ''):
                                continue
                            P.dma("sp", QTs[g][lq], qs[g, 128 * pi:128 * pi + 128, T * t:T * t + T],
                                  ["qs"], ["QT%d_%d" % (g, lq)])
                        P.dma("sp", SZ[lb], zs[128 * pi:128 * pi + 128, T * t:T * t + T],
                              ["zs"], ["SZ%d" % lb])

                    loads_list.append(loads)
                    obanks = []
                    first_item_of_tile = True
                    for xi, h in enumerate((ha, hb)):
                        isB = xi == 1
                        R = slice(64, 128) if isB else slice(0, 64)
                        ob = 4 + (st["o"] % 4)
                        st["o"] += 1
                        obanks.append(ob)
                        banks = []
                        for g in range(ng):
                            if kind == 1 and str(g) not in os.environ.get('K_GRP', '012'):
                                continue
                            dil = DIL[g][1] if kind == 1 else 1
                            nq = T // dil
                            slots = []
                            if nq >= 128:
                                nb = 32 // dil
                                for r in range(dil):
                                    for bq in range(nq // 128):
                                        fb = (T * t // dil) // 128 + bq
                                        qsl = slice(r + dil * 128 * bq, r + dil * 128 * bq + dil * 127 + 1, dil) if dil > 1 \
                                            else slice(128 * bq, 128 * bq + 128)
                                        for which, kb in ((0, fb - 1), (1, fb)):
                                            valid = kb >= 0
                                            kbb = kb if valid else fb
                                            ksl = slice(r + dil * 128 * kbb, r + dil * 128 * kbb + dil * 127 + 1, dil) if dil > 1 \
                                                else slice(128 * kbb, 128 * kbb + 128)
                                            slots.append((g, ksl, qsl, r * nb + kbb, valid, which, 128, 0))
                            else:
                                bb = (T * t) // 2048
                                u = ((T * t) % 2048) // T
                                nb = 2
                                for r in range(dil):
                                    qsl = slice(r, r + dil * (nq - 1) + 1, dil)
                                    for which, kb in ((0, bb - 1), (1, bb)):
                                        valid = kb >= 0
                                        kbb = kb if valid else bb
                                        ksl = slice(r + 2048 * kbb, r + 2048 * kbb + dil * 127 + 1, dil)
                                        slots.append((g, ksl, qsl, r * nb + kbb, valid, which, nq, nq * u))
                            cur, used = [], 0
                            for sl in slots:
                                if used + sl[6] > 512:
                                    banks.append(cur)
                                    cur, used = [], 0
                                cur.append(sl + (used,))
                                used += sl[6]
                            if cur:
                                banks.append(cur)
                        for bi, bank in enumerate(banks):
                            def qk(bank=bank, R=R, lb=lq, first=(bi == 0), ob=ob, h=h, do_loads=first_item_of_tile,
                                   ti=ti, kvl=(kvload if (flush_before and first_item_of_tile) else None),
                                   _pi=pi, _xi=xi, _t=t):
                                if kvl is not None:
                                    kvl()
                                if do_loads:
                                    if ti == 0:
                                        loads_list[0]()
                                    if ti + 1 < len(loads_list):
                                        loads_list[ti + 1]()
                                if os.environ.get('K_PTD') == '1' and _pi == 0 and _xi == 0 and first:
                                    P.dma("pool", q32[256:384, T * _t:T * _t + T], QTs[0][lb],
                                          ["QT0_%d" % lb], ["q32"])
                                    if _t == 0:
                                        P.dma("pool", q32[384:512, :], KTs[0], ["KT0"], ["q32"])
                                sbk = st["s"] % 3
                                st["s"] += 1
                                if first:
                                    ih = h if kind == 0 else 0
                                    P.mm(ps[ob], initrow[0:1, 128 * ih:128 * ih + 128], onesrow[0:1, :], True, False,
                                         ["initrow", "onesrow"], ["ps%d" % ob])
                                for (g, ksl, qsl, kt, valid, which, w, moff, col) in bank:
                                    P.mm(ps[sbk][:, col:col + w], KTs[g][R, ksl], QTs[g][lb][R, qsl], True, True,
                                         ["KT%d" % g, "QT%d_%d" % (g, lb)], ["ps%d" % sbk])
                                return sbk

                            def sm(sbk, bank=bank, _pi=pi, _xi=xi, _bi=bi, _t=t):
                                pt = st["pt"] % NPT
                                st["pt"] += 1
                                ncol = sum(sl[6] for sl in bank)
                                P.act(PT[pt][:, 0:ncol], ps[sbk][:, 0:ncol], AF.Exp, ["ps%d" % sbk], ["PT%d" % pt],
                                      scale=SCALE)
                                w = bank[0][6]
                                if w == 128:
                                    P.tt("dve", PT[pt][:, 0:ncol], PT[pt][:, 0:ncol], mask4[:, 0:ncol], ALU.mult,
                                         ["PT%d" % pt, "mask4"], ["PT%d" % pt])
                                else:
                                    moff = bank[0][7]
                                    nsl = len(bank) // 2
                                    mv = mask4[:, 0:256].rearrange("p (a c) -> p a c", a=2)[:, :, moff:moff + w]
                                    for half in range(0, nsl, 1):
                                        pass
                                    pv_ = PT[pt][:, 0:ncol].rearrange("p (s a c) -> p s a c", a=2, c=w)
                                    P.op("dve", lambda e, pv_=pv_, mv=mv, nsl=nsl, w=w: e.tensor_tensor(
                                        pv_, pv_, mv.unsqueeze(1).to_broadcast([128, nsl, 2, w]), ALU.mult),
                                        ["PT%d" % pt, "mask4"], ["PT%d" % pt])
                                if os.environ.get('K_PTD') == '1' and _pi == 0 and _xi == 0:
                                    P.dma("pool", q32[128 * _bi:128 * _bi + 128, T * _t:T * _t + T], PT[pt],
                                          ["PT%d" % pt], ["q32"])
                                return pt

                            def pv(pt, bank=bank, isB=isB, ob=ob, last=(bi == len(banks) - 1)):
                                vb = [sl for sl in bank if sl[4]]
                                for i, (g, ksl, qsl, kt, valid, which, w, moff, col) in enumerate(vb):
                                    P.mm(ps[ob][:, qsl], vaug(Vts[g], kt, isB), PT[pt][:, col:col + w], False,
                                         last and i == len(vb) - 1,
                                         ["Vt%d" % g, "PT%d" % pt], ["ps%d" % ob])

                            items.append([qk, sm, pv, None, flush_before and first_item_of_tile])
                            first_item_of_tile = False
                    oA, oB = obanks
                    items[-1][3] = (lambda pi=pi, t=t, oA=oA, oB=oB, lb=lb: epilogue(pi, t, oA, oB, lb))
        else:
            mown = A.alloc(4 * 512)
            P.dma("pool", mown, mown_d, [], ["mown"])
            ident = A.alloc(128)
            P.dma("pool", ident, ident_d, [], ["ident"])
            KX = [A.alloc(S) for _ in range(2)]
            Vt = A.alloc(VW)
            QX = [[A.alloc(512) for _ in range(3)] for _ in range(2)]
            Q32 = [[A.alloc(512, F32) for _ in range(3)] for _ in range(2)]
            ksA = [A.alloc(16, F32) for _ in range(2)]
            gsb = A.alloc(64, F32)
            top8 = A.alloc(32, F32)
            selb = A.alloc(64, F32)
            biasp = A.alloc(4 * 80)
            P.memset("dve", biasp, 0.0, ["biasp"])
            psT = ps[3].bitcast(BF16)
            cur_kp = -1
            for pi, (ha, hb, kp) in enumerate(pairs):
                newkp = kp != cur_kp
                cur_kp = kp

                def kvload(kp=kp):
                    load_v(0, kp, Vt, 1, "m")
                    for xi in range(2):
                        P.dma("sp", KX[xi][0:64, :], ks[0, 128 * kp + 64 * xi:128 * kp + 64 * xi + 64, :],
                              ["ks"], ["KX%d" % xi])
                        P.dma("pool", KX[xi][64:80, :], erow_d, [], ["KX%d" % xi])
                        P.dma("sp", ksA[xi][0:64, :], ksum_d[128 * kp + 64 * xi:128 * kp + 64 * xi + 64, :],
                              ["ksum_d"], ["ksA%d" % xi])
                for t in range(NT):
                    ti = pi * NT + t
                    lb = ti % 3
                    flush_before = newkp and t == 0

                    def loads(pi=pi, t=t, lb=lb):
                        for xi in range(2):
                            r0 = 128 * pi + 64 * xi
                            P.dma("sp", QX[xi][lb][0:64, :], qs[0, r0:r0 + 64, T * t:T * t + T],
                                  ["qs"], ["QX%d_%d" % (xi, lb)])
                            P.dma("sp", Q32[xi][lb][0:64, :], q32[r0:r0 + 64, T * t:T * t + T],
                                  ["q32"], ["Q32%d_%d" % (xi, lb)])
                        P.dma("sp", SZ[lb], zs[128 * pi:128 * pi + 128, T * t:T * t + T],
                              ["zs"], ["SZ%d" % lb])

                    loads_list.append(loads)

                    def gate(xi, t=t, lb=lb):
                        gp = 2
                        for j in range(4):
                            P.mm(ps[gp][:, 16 * j:16 * j + 16], Q32[xi][lb][0:64, 128 * j:128 * j + 128],
                                 ksA[xi][0:64, :], True, True,
                                 ["Q32%d_%d" % (xi, lb), "ksA%d" % xi], ["ps%d" % gp])
                        g3 = gsb.rearrange("p (j n) -> p j n", j=4)
                        s3 = selb.rearrange("p (j n) -> p j n", j=4)
                        t3 = top8.rearrange("p (j n) -> p j n", j=4)
                        b3 = biasp.rearrange("p (j n) -> p j n", j=4)
                        P.memset("dve", gsb, -1e30, ["gsb"])
                        for j in range(4):
                            own = 2 * t + j // 2
                            if own > 0:
                                P.cp("dve", g3[:, j, 0:own], ps[gp][:, 16 * j:16 * j + own], ["ps%d" % gp], ["gsb"])
                        P.memset("dve", selb, -BIG, ["selb"])
                        for j in range(4):
                            own = 2 * t + j // 2
                            if own > 3:
                                P.max8("dve", t3[:, j, :], g3[:, j, :], ["gsb"], ["top8"])
                                P.ts("dve", s3[:, j, 0:own], g3[:, j, 0:own], t3[:, j, 2:3], None, ALU.is_ge, None,
                                     ["gsb", "top8", "selb"], ["selb"])
                                P.ts("dve", s3[:, j, 0:own], s3[:, j, 0:own], BIG, -BIG, ALU.mult, ALU.add,
                                     ["selb"], ["selb"])
                                P.memset("dve", s3[:, j, own:own + 1], 0.0, ["selb"])
                            else:
                                P.memset("dve", s3[:, j, 0:own + 1], 0.0, ["selb"])
                        P.cp("dve", b3[:, :, 64:80], s3, ["selb", "biasp"], ["biasp"])
                        for j in range(4):
                            P.tr(psT[0:80, 128 * j:128 * j + 128], b3[:, j, :], ident, ["biasp", "ident"], ["ps3"])
                        P.cp("dve", QX[xi][lb][64:80, :], psT[64:80, 0:512], ["ps3"], ["QX%d_%d" % (xi, lb)])

                    obanks = []
                    first_item_of_tile = True
                    for xi, h in enumerate((ha, hb)):
                        isB = xi == 1
                        ob = 4 + (st["o"] % 4)
                        st["o"] += 1
                        obanks.append(ob)
                        nkt = 4 * (t + 1)
                        for kt in range(nkt):
                            def qk(kt=kt, xi=xi, lb=lb, ob=ob, h=h, do_loads=first_item_of_tile, gate=gate,
                                   ti=ti, kvl=(kvload if (flush_before and first_item_of_tile) else None)):
                                if kvl is not None:
                                    kvl()
                                if do_loads:
                                    if ti == 0:
                                        loads_list[0]()
                                    if ti + 1 < len(loads_list):
                                        loads_list[ti + 1]()
                                if kt == 0:
                                    gate(xi)
                                    P.mm(ps[ob], initrow[0:1, 0:128], onesrow[0:1, :], True, False,
                                         ["initrow", "onesrow"], ["ps%d" % ob])
                                sbk = st["s"] % 2
                                st["s"] += 1
                                P.mm(ps[sbk], KX[xi][0:80, 128 * kt:128 * kt + 128], QX[xi][lb][0:80, :], True, True,
                                     ["KX%d" % xi, "QX%d_%d" % (xi, lb)], ["ps%d" % sbk])
                                return sbk

                            def sm(sbk, kt=kt, t=t):
                                pt = st["pt"] % 4
                                st["pt"] += 1
                                P.act(PT[pt], ps[sbk], AF.Exp, ["ps%d" % sbk], ["PT%d" % pt], scale=SCALE)
                                if kt >= 4 * t:
                                    m = mown[:, 512 * (kt - 4 * t):512 * (kt - 4 * t) + 512]
                                    P.tt("pool", PT[pt], PT[pt], m, ALU.mult, ["PT%d" % pt, "mown"], ["PT%d" % pt])
                                return pt

                            def pv(pt, kt=kt, isB=isB, ob=ob, last=(kt == nkt - 1)):
                                P.mm(ps[ob], vaug(Vt, kt, isB), PT[pt], False, last,
                                     ["Vtm", "PT%d" % pt], ["ps%d" % ob])

                            items.append([qk, sm, pv, None, flush_before and first_item_of_tile])
                            first_item_of_tile = False
                    oA, oB = obanks
                    items[-1][3] = (lambda pi=pi, t=t, oA=oA, oB=oB, lb=lb: epilogue(pi, t, oA, oB, lb))

        if _skip:
            items = []
        prev = None
        for it in items:
            if it[4] and prev is not None:
                prev[0][2](prev[1])
                if prev[0][3] is not None:
                    prev[0][3]()
                prev = None
            sbk = it[0]()
            if prev is not None:
                prev[0][2](prev[1])
                if prev[0][3] is not None:
                    prev[0][3]()
            pt = it[1](sbk)
            prev = (it, pt)
        if prev is not None:
            prev[0][2](prev[1])
            if prev[0][3] is not None:
                prev[0][3]()
        P.barrier()
        if stop == 'A' and li == NL - 1:
            for c in range(8):
                if os.environ.get('K_XHD') == '1':
                    P.dma("pool", outT[128 * c:128 * c + 128, :], XH[:, c, :], [], ["outT"])
                elif os.environ.get('K_PTD') == '1':
                    P.dma("sp", outT[128 * c:128 * c + 128, :], q32[128 * c:128 * c + 128, :], [], ["outT"])
                else:
                    P.dma("pool", outT[128 * c:128 * c + 128, :], ys[128 * c:128 * c + 128, :], [], ["outT"])
            return _finish(nc, P)

        A.reset()
        Yb = [A.alloc(8 * 512) for _ in range(2)]
        Wo = A.alloc(8 * 1024)
        T32 = [A.alloc(8 * 512, F32) for _ in range(2)]
        SQ = [A.alloc(512, F32) for _ in range(2)]
        mean_sb = A.alloc(512, F32)
        rstd_sb = A.alloc(512, F32)
        m2 = A.alloc(512, F32)
        onesM = A.alloc(128, F32)
        G = A.alloc(8, F32)
        Bt = A.alloc(8, F32)
        P.memset("dve", onesM, 1.0 / D, ["onesM"])
        P.dma("sp", G, lng[li], [], ["G"])
        P.dma("sp", Bt, lnb[li], [], ["Bt"])
        Wo3 = Wo.rearrange("p (c n) -> p c n", c=8)
        for pi, (ha, hb, kp) in enumerate(pairs):
            P.dma("pool", Wo3[0:64, pi, :], w_out[li][64 * ha:64 * ha + 64, :], [], ["Wo"])
            P.dma("pool", Wo3[64:128, pi, :], w_out[li][64 * hb:64 * hb + 64, :], [], ["Wo"])
        last_layer = li == NL - 1
        for t in range(NT):
            yb = t % 2
            Y3 = Yb[yb].rearrange("p (c n) -> p c n", c=8)
            P.dma("sp", Y3, ys.rearrange("(c p) s -> p c s", p=128)[:, :, T * t:T * t + T], ["ys"], ["Y%d" % yb])
            T3 = T32[yb].rearrange("p (c n) -> p c n", c=8)
            tr_ = "T32_%d" % yb
            for j in range(8):
                pb = j % 3
                for c in range(8):
                    P.mm(ps[pb], Wo3[:, c, 128 * j:128 * j + 128], Y3[:, c, :], c == 0, c == 7,
                         ["Wo", "Y%d" % yb], ["ps%d" % pb])
                P.stt("dve", T3[:, j, :], XH[:, j, T * t:T * t + T], ALPHA, ps[pb], ALU.mult, ALU.add,
                      ["ps%d" % pb, xres(j, t) + "h"], [tr_ + "_%d" % j])
                P.stt("dve", T3[:, j, :], XL[:, j, T * t:T * t + T], ALPHA, T3[:, j, :], ALU.mult, ALU.add,
                      [tr_ + "_%d" % j, xres(j, t) + "l"], [tr_ + "_%d" % j])
            _ol = int(os.environ.get('K_OLVL', '9'))
            if _ol == 1:
                continue
            for j in range(8):
                P.mm(ps[3], onesM, T3[:, j, :], j == 0, j == 7, ["onesM", tr_ + "_%d" % j], ["ps3"])
            for j in range(8):
                sq = j % 2
                P.act(SQ[sq], T3[:, j, :], AF.Square, [tr_ + "_%d" % j], ["SQ%d" % sq])
                P.mm(ps[4], onesM, SQ[sq], j == 0, j == 7, ["onesM", "SQ%d" % sq], ["ps4"])
            if _ol == 2:
                continue
            P.cp("act", mean_sb, ps[3], ["ps3"], ["mean"])
            P.tt("dve", m2, mean_sb, mean_sb, ALU.mult, ["mean"], ["m2"])
            P.tt("dve", m2, ps[4], m2, ALU.subtract, ["ps4", "m2"], ["m2"])
            P.ts("dve", m2, m2, EPS, None, ALU.add, None, ["m2"], ["m2"])
            P.act(rstd_sb, m2, AF.Sqrt, ["m2"], ["rstd"])
            P.recip("dve", rstd_sb, rstd_sb, ["rstd"], ["rstd"])
            if _ol == 3:
                continue
            for j in range(8):
                rn = tr_ + "_%d" % j
                P.tt("dve", T3[:, j, :], T3[:, j, :], mean_sb, ALU.subtract, [rn, "mean"], [rn])
                P.tt("pool", T3[:, j, :], T3[:, j, :], rstd_sb, ALU.mult, [rn, "rstd"], [rn])
                P.ts("dve", T3[:, j, :], T3[:, j, :], G[:, j:j + 1], Bt[:, j:j + 1], ALU.mult, ALU.add,
                     [rn, "G", "Bt"], [rn])
                if last_layer:
                    P.dma("sp", outT[128 * j:128 * j + 128, T * t:T * t + T], T3[:, j, :], [rn], ["outT"])
                else:
                    P.cp("act", XH[:, j, T * t:T * t + T], T3[:, j, :], [rn], [xres(j, t) + "h"])
                    P.tt("pool", XL[:, j, T * t:T * t + T], T3[:, j, :], XH[:, j, T * t:T * t + T], ALU.subtract,
                         [rn, xres(j, t) + "h"], [xres(j, t) + "l"])
        P.barrier()
        if stop == 'O%d' % li:
            for j in range(8):
                src = XL if os.environ.get('K_XL') == '1' else XH
                P.dma("pool", outT[128 * j:128 * j + 128, :], src[:, j, :], [], ["outT"])
            return _finish(nc, P)

    return _finish(nc, P)


def _finish(nc, P):
    with nc.allow_low_precision("bf16 matmul operands, fp32 accumulation"):
        with nc.allow_non_contiguous_dma("layout"):
            P.emit()
    return nc


def _consts():
    pos = np.arange(S, dtype=np.float32)
    inv = (np.float32(500000.0) ** (-np.arange(0, 16, 2, dtype=np.float32) / np.float32(16))).astype(np.float32)
    ang = pos[None, :] * inv[:, None]
    cos = np.cos(ang).astype(np.float32)
    sin = np.sin(ang).astype(np.float32)
    C = np.ones((128, S), np.float32)
    Sn = np.zeros((128, S), np.float32)
    permT = np.zeros((128, 128), np.float32)
    for base in (0, 64):
        C[base:base + 8] = cos
        C[base + 8:base + 16] = cos
        Sn[base:base + 8] = sin
        Sn[base + 8:base + 16] = sin
        for i in range(8):
            permT[base + i + 8, base + i] = -1.0
            permT[base + i, base + i + 8] = 1.0
    j = np.arange(128)[:, None]
    r = np.arange(128)[None, :]
    cur = (j <= r).astype(np.float32)
    prev_swa = (j >= r + 1).astype(np.float32)
    prev_dil = (j >= r).astype(np.float32)
    mswa = np.concatenate([prev_swa, cur, prev_swa, cur], axis=1)
    mdil = np.concatenate([prev_dil, cur, prev_dil, cur], axis=1)
    one = np.ones((128, 128), np.float32)
    zero = np.zeros((128, 128), np.float32)
    mown = np.concatenate([
        cur, one, one, one,
        zero, cur, one, one,
        zero, zero, cur, one,
        zero, zero, zero, cur], axis=1)
    erows = np.zeros((16, S), np.float32)
    for n in range(16):
        erows[n, 256 * n:256 * n + 256] = 1.0
    ident = np.eye(128, dtype=np.float32)
    return dict(ropeC=C, ropeS=Sn, permT=permT, mask_swa=mswa, mask_dil=mdil, mask_own=mown,
                erows=erows, ident=ident)


_NC_CACHE = {}


def make_in_maps(inputs, NL, cores):
    consts = _consts()
    maps = []
    for c in cores:
        m = dict(consts)
        m["xT"] = np.ascontiguousarray(np.asarray(inputs["x"][c], dtype=np.float32).T)
        for i in range(NL):
            m["w_in_%d" % i] = np.ascontiguousarray(inputs["w_in_%d" % i], dtype=np.float32)
            m["w_out_%d" % i] = np.ascontiguousarray(inputs["w_out_%d" % i], dtype=np.float32)
            m["lng_%d" % i] = np.ascontiguousarray(np.asarray(inputs["ln_g_%d" % i], np.float32).reshape(8, 128).T)
            m["lnb_%d" % i] = np.ascontiguousarray(np.asarray(inputs["ln_b_%d" % i], np.float32).reshape(8, 128).T)
            if i % 3 == 0:
                m["sink_%d" % i] = np.asarray(inputs["sink_%d" % i], np.float32).reshape(1, 16)
        maps.append(m)
    return maps


def kernel(x, w_in_0, sink_0, w_out_0, ln_g_0, ln_b_0,
           w_in_1, w_out_1, ln_g_1, ln_b_1,
           w_in_2, w_out_2, ln_g_2, ln_b_2,
           w_in_3, sink_3, w_out_3, ln_g_3, ln_b_3):
    inputs = dict(x=x, w_in_0=w_in_0, sink_0=sink_0, w_out_0=w_out_0, ln_g_0=ln_g_0, ln_b_0=ln_b_0,
                  w_in_1=w_in_1, w_out_1=w_out_1, ln_g_1=ln_g_1, ln_b_1=ln_b_1,
                  w_in_2=w_in_2, w_out_2=w_out_2, ln_g_2=ln_g_2, ln_b_2=ln_b_2,
                  w_in_3=w_in_3, sink_3=sink_3, w_out_3=w_out_3, ln_g_3=ln_g_3, ln_b_3=ln_b_3)
    NL = 4
    if NL not in _NC_CACHE:
        _NC_CACHE[NL] = build_program(NL)
    nc = _NC_CACHE[NL]
    cores = list(range(8))
    in_maps = make_in_maps(inputs, NL, cores)
    res = run_bass_kernel_spmd(nc, in_maps, core_ids=cores)
    out = np.stack([np.asarray(r["outT"], dtype=np.float32).T for r in res.results], axis=0)
    return np.ascontiguousarray(out)
```

```python
import numpy as np
import concourse.bass as bass
import concourse.mybir as mybir
from concourse.ap import AP
from concourse.bass_utils import run_bass_kernel_spmd

F32 = mybir.dt.float32
BF16 = mybir.dt.bfloat16
AF = mybir.ActivationFunctionType
ALU = mybir.AluOpType
AX = mybir.AxisListType

S = 4096
D = 1024
T = 512
NT = S // T
ALPHA = 8.0 ** 0.25
EPS = 1e-5
SCALE = 0.125
BIG = 30000.0
DIL = ((128, 1), (512, 4), (2048, 16))


class Prog:
    def __init__(self, nc):
        self.nc = nc
        self.engs = ["pe", "act", "dve", "pool", "sp"]
        self.streams = {e: [] for e in self.engs}
        self.count = {e: 0 for e in self.engs}
        self.semh = {}
        self.dma_n = {e: 0 for e in self.engs}
        self.NDS = 8
        self.waited = {e: {} for e in self.engs}
        self.lastw = {}
        self.readers = {}

    def sem(self, key):
        if key not in self.semh:
            self.semh[key] = self.nc.alloc_semaphore(key)
        return self.semh[key]

    def op(self, eng, fn, reads=(), writes=(), dma=False):
        writes = list(writes) + [r for r in reads if r.startswith("ps") and r not in writes]
        reads = [r for r in reads if not r.startswith("ps")]
        deps = {}

        def add(tok):
            if tok is None:
                return
            k, v, e = tok
            if e == eng and eng == "pe" and k == "c_pe":
                return
            if deps.get(k, 0) < v:
                deps[k] = v

        for r in reads:
            add(self.lastw.get(r))
        for w in writes:
            add(self.lastw.get(w))
            for tok in self.readers.get(w, {}).values():
                add(tok)
        if dma:
            i = self.dma_n[eng]
            self.dma_n[eng] += 1
            k = "d_%s_%d" % (eng, i % self.NDS)
            val = 16 * (i // self.NDS + 1)
            if val > 16:
                if deps.get(k, 0) < val - 16:
                    deps[k] = val - 16
            tok = (k, val, eng)
        else:
            self.count[eng] += 1
            tok = ("c_" + eng, self.count[eng], eng)
        waits = []
        for k, v in deps.items():
            if self.waited[eng].get(k, 0) < v:
                self.waited[eng][k] = v
                waits.append((k, v))
        self.streams[eng].append((waits, fn, tok[0], 16 if dma else 1))
        for r in reads:
            d = self.readers.setdefault(r, {})
            if tok[0] not in d or d[tok[0]][1] < tok[1]:
                d[tok[0]] = tok
        for w in writes:
            self.lastw[w] = tok
            self.readers[w] = {}
        return tok

    def mm(self, out, lhsT, rhs, start, stop, reads, writes):
        self.op("pe", lambda e: e.matmul(out, lhsT, rhs, start=start, stop=stop,
                                         skip_group_check=True), reads, writes)

    def tr(self, out, in_, ident, reads, writes):
        self.op("pe", lambda e: e.transpose(out, in_, ident), reads, writes)

    def act(self, out, in_, func, reads, writes, **kw):
        self.op("act", lambda e: e.activation(out, in_, func, **kw), reads, writes)

    def tt(self, eng, out, in0, in1, op, reads, writes):
        self.op(eng, lambda e: e.tensor_tensor(out, in0, in1, op), reads, writes)

    def ts(self, eng, out, in0, s1, s2, op0, op1, reads, writes):
        if s2 is None:
            self.op(eng, lambda e: e.tensor_scalar(out, in0, s1, None, op0), reads, writes)
        else:
            self.op(eng, lambda e: e.tensor_scalar(out, in0, s1, s2, op0, op1), reads, writes)

    def stt(self, eng, out, in0, scalar, in1, op0, op1, reads, writes):
        self.op(eng, lambda e: e.scalar_tensor_tensor(out, in0, scalar, in1, op0, op1),
                reads, writes)

    def cp(self, eng, out, in_, reads, writes):
        if eng == "act":
            self.op(eng, lambda e: e.activation(out, in_, AF.Copy), reads, writes)
        else:
            self.op(eng, lambda e: e.tensor_copy(out, in_), reads, writes)

    def recip(self, eng, out, in_, reads, writes):
        self.op(eng, lambda e: e.reciprocal(out, in_), reads, writes)

    def reduce_add(self, eng, out, in_, reads, writes):
        self.op(eng, lambda e: e.tensor_reduce(out, in_, AX.X, ALU.add), reads, writes)

    def max8(self, eng, out, in_, reads, writes):
        self.op(eng, lambda e: e.max(out, in_), reads, writes)

    def memset(self, eng, ap, val, writes):
        self.op(eng, lambda e: e.memset(ap, val), (), writes)

    def dma(self, q, out, in_, reads, writes):
        self.op(q, lambda e: e.dma_start(out=out, in_=in_), reads, writes, dma=True)

    def barrier(self):
        allres = list(self.lastw.keys() | self.readers.keys())
        for e in ["pe", "act", "dve", "pool", "sp"]:
            pass
        self._bar = getattr(self, "_bar", 0) + 1
        toks = {}
        for r in allres:
            t = self.lastw.get(r)
            if t is not None and toks.get(t[0], (0,))[0] < t[1]:
                toks[t[0]] = (t[1], t[2])
            for t in self.readers.get(r, {}).values():
                if toks.get(t[0], (0,))[0] < t[1]:
                    toks[t[0]] = (t[1], t[2])
        for e in self.engs:
            waits = []
            for k, (v, te) in toks.items():
                if te == e and e == "pe" and k == "c_pe":
                    continue
                if self.waited[e].get(k, 0) < v:
                    self.waited[e][k] = v
                    waits.append((k, v))
            if waits:
                self.streams[e].append((waits, None, None, 0))
        self.lastw = {}
        self.readers = {}

    def emit(self):
        nc = self.nc
        self.barrier()
        emap = {"pe": "tensor", "act": "scalar", "dve": "vector", "pool": "gpsimd", "sp": "sync"}
        with nc.Block() as block:
            for eng in self.engs:
                stream = self.streams[eng]

                def body(e, stream=stream):
                    for waits, fn, inc, amt in stream:
                        for k, v in waits:
                            e.wait_ge(self.sem(k), v)
                        if fn is not None:
                            fn(e).then_inc(self.sem(inc), amt)

                getattr(block, emap[eng])(body)


class Arena:
    def __init__(self, t, ncols):
        self.t = t
        self.n = ncols
        self.off = 0

    def reset(self):
        self.off = 0

    def alloc(self, cols, dtype=BF16):
        w = cols * (2 if dtype == F32 else 1)
        self.off = (self.off + 15) // 16 * 16
        a = self.off
        self.off += w
        assert self.off <= self.n, ("arena overflow", self.off, self.n)
        ap = self.t[:, a:a + w]
        if dtype == F32:
            ap = ap.bitcast(F32)
        return ap


def layer_cols(kind):
    if kind == 0:
        return [(0, 1024, 1152)], 1280, 2
    if kind == 1:
        return [(g * 1536, g * 1536 + 1024, g * 1536 + 1280) for g in range(3)], 4608, 4
    return [(0, 1024, 1280)], 1536, 4


def head_pairs(kind):
    if kind == 0:
        return [(h, h + 8, 0) for h in range(8)]
    return [(h, h + 4, 0) for h in range(4)] + [(h, h + 4, 1) for h in range(8, 12)]


def build_program(NL=4, dbg_mix=False, stop=None):
    import os
    stop = stop or os.environ.get('K_STOP')
    nc = bass.Bass("TRN2", target_bir_lowering=False)
    P = Prog(nc)
    kinds = [i % 3 for i in range(NL)]
    def din(name, shape):
        return nc.dram_tensor(name, shape, F32, kind="ExternalInput").ap()

    xT = din("xT", [D, S])
    outT = nc.dram_tensor("outT", [D, S], F32, kind="ExternalOutput").ap()
    w_in, w_out, lng, lnb, sink = [], [], [], [], []
    for i in range(NL):
        ncols = {0: 2304, 1: 5632, 2: 2560}[kinds[i]]
        w_in.append(din("w_in_%d" % i, [D, ncols]))
        w_out.append(din("w_out_%d" % i, [D, D]))
        lng.append(din("lng_%d" % i, [128, 8]))
        lnb.append(din("lnb_%d" % i, [128, 8]))
        sink.append(din("sink_%d" % i, [1, 16]) if kinds[i] == 0 else None)
    ropeC_d = din("ropeC", [128, S])
    ropeS_d = din("ropeS", [128, S])
    perm_d = din("permT", [128, 128])
    mswa_d = din("mask_swa", [128, 512])
    mdil_d = din("mask_dil", [128, 512])
    mown_d = din("mask_own", [128, 4 * 512])
    erow_d = din("erows", [16, S])
    ident_d = din("ident", [128, 128])

    def dscr(name, shape, dt=BF16):
        return nc.dram_tensor(name, shape, dt, kind="Internal").ap()

    qs = dscr("qs", [3, D, S])
    ks = dscr("ks", [3, 256, S])
    vs = dscr("vs", [3, S, 256])
    zs = dscr("zs", [D, S])
    ys = dscr("ys", [D, S])
    q32 = dscr("q32", [D, S], F32)
    ksum_d = dscr("ksum_d", [256, 16], F32)

    XH = nc.alloc_sbuf_tensor("XH", [128, 8, S], BF16)
    XL = nc.alloc_sbuf_tensor("XL", [128, 8, S], BF16)
    ACOLS = 40448
    arena_t = nc.alloc_sbuf_tensor("arena", [128, ACOLS], BF16)
    A = Arena(arena_t, ACOLS)
    ps = [nc.alloc_psum_tensor("ps%d" % i, [128, 512], F32)[:, :] for i in range(8)]

    def xres(j, t):
        return "X_%d_%d" % (j, t)

    A.reset()
    stg = [A.alloc(512, F32) for _ in range(3)]
    n = 0
    for t in range(NT):
        for j in range(8):
            b = n % 3
            n += 1
            P.dma("sp", stg[b], xT[128 * j:128 * j + 128, T * t:T * t + T], [], ["stg%d" % b])
            P.cp("act", XH[:, j, T * t:T * t + T], stg[b], ["stg%d" % b], [xres(j, t) + "h"])
            P.tt("dve", XL[:, j, T * t:T * t + T], stg[b], XH[:, j, T * t:T * t + T],
                 ALU.subtract, ["stg%d" % b, xres(j, t) + "h"], [xres(j, t) + "l"])
    P.barrier()
    if stop == 'load':
        return _finish(nc, P)

    for li in range(NL):
        kind = kinds[li]
        groups, zbase, nkv = layer_cols(kind)
        pairs = head_pairs(kind)
        nkvp = nkv // 2
        ng = len(groups)
        wv = w_in[li].rearrange("(kc p) n -> p kc n", p=128)

        _skip = os.environ.get('K_SKIP_PA') == '1'
        A.reset()
        ropeC = A.alloc(S, F32)
        ropeS = A.alloc(S, F32)
        Wb = [A.alloc(1024) for _ in range(3)]
        permT = A.alloc(128)
        qb = [A.alloc(512) for _ in range(2)]
        t1 = [A.alloc(512, F32) for _ in range(2)]
        t2 = [A.alloc(512, F32) for _ in range(2)]
        qr = [A.alloc(512, F32) for _ in range(2)]
        stb = [A.alloc(512) for _ in range(3)]
        ksum = A.alloc(32, F32)
        P.dma("sp", ropeC, ropeC_d, [], ["ropeC"])
        P.dma("sp", ropeS, ropeS_d, [], ["ropeS"])
        P.dma("pool", permT, perm_d, [], ["permT"])

        chunks = []
        for g, (qbs, kbs, vbs) in enumerate(groups):
            for kp in range(nkvp):
                chunks.append(("K", g, kp, [(kbs + 128 * kp, 128, 0)]))
                chunks.append(("V", g, kp, [(vbs + 128 * kp, 128, 0)]))
        for pi, (ha, hb, kp) in enumerate(pairs):
            for g, (qbs, kbs, vbs) in enumerate(groups):
                chunks.append(("Q", g, pi, [(qbs + 64 * ha, 64, 0), (qbs + 64 * hb, 64, 64)]))
            chunks.append(("Z", 0, pi, [(zbase + 64 * ha, 64, 0), (zbase + 64 * hb, 64, 64)]))

        cnt = {"w": 0, "ps": 0, "pp": 0, "qb": 0, "t": 0, "st": 0}
        _kt = os.environ.get('K_TYPES')
        if _kt:
            chunks = [c for c in chunks if c[0] in _kt][:int(os.environ.get('K_NCH', '100'))]
        pending = []

        def flush_pending():
            while pending:
                pending.pop(0)()

        if _skip:
            chunks = []
        for (typ, g, idx, srcs) in chunks:
            wi = cnt["w"] % 3
            cnt["w"] += 1
            W = Wb[wi].rearrange("p (kc n) -> p kc n", kc=8)
            for (c0, ncs, d0) in srcs:
                P.dma("pool", W[:, :, d0:d0 + ncs], wv[:, :, c0:c0 + ncs], [], ["W%d" % wi])
            if typ == "V":
                for s4 in range(8):
                    pb = cnt["ps"] % 3
                    cnt["ps"] += 1
                    for ss in range(4):
                        s = 4 * s4 + ss
                        for kc in range(8):
                            P.mm(ps[pb][:, 128 * ss:128 * ss + 128], XH[:, kc, 128 * s:128 * s + 128],
                                 W[:, kc, :], kc == 0, kc == 7,
                                 ["W%d" % wi, xres(kc, s // 4) + "h"], ["ps%d" % pb])
                    sb = cnt["st"] % 3
                    cnt["st"] += 1
                    P.cp("act", stb[sb], ps[pb], ["ps%d" % pb], ["stb%d" % sb])
                    dst = vs[g].rearrange("(s p) c -> p s c", p=128)[:, 4 * s4:4 * s4 + 4, 128 * idx:128 * idx + 128]
                    P.dma("sp", dst, stb[sb].rearrange("p (s c) -> p s c", s=4), ["stb%d" % sb], ["vs"])
                continue
            for t in range(NT):
                pb = cnt["ps"] % 3
                cnt["ps"] += 1
                for kc in range(8):
                    P.mm(ps[pb], W[:, kc, :], XH[:, kc, T * t:T * t + T], kc == 0, kc == 7,
                         ["W%d" % wi, xres(kc, t) + "h"], ["ps%d" % pb])
                flush_pending()
                if typ == "Z":
                    sb = cnt["st"] % 3
                    cnt["st"] += 1
                    P.act(stb[sb], ps[pb], AF.Silu, ["ps%d" % pb], ["stb%d" % sb])
                    P.dma("sp", zs[128 * idx:128 * idx + 128, T * t:T * t + T], stb[sb],
                          ["stb%d" % sb], ["zs"])
                    continue
                qi = cnt["qb"] % 2
                cnt["qb"] += 1
                P.cp("act", qb[qi], ps[pb], ["ps%d" % pb], ["qb%d" % qi])
                _lvl = int(os.environ.get('K_LVL', '9'))
                if _lvl == 1:
                    P.dma("sp", ks[g, 128 * idx:128 * idx + 128, T * t:T * t + T], qb[qi], ["qb%d" % qi], ["ks"])
                    continue
                P.tt("dve", t1[qi], ps[pb], ropeC[:, T * t:T * t + T], ALU.mult,
                     ["ps%d" % pb, "ropeC"], ["t1_%d" % qi])
                if _lvl == 2:
                    continue
                if _lvl == 3:
                    pp = 3 + cnt["pp"] % 2
                    cnt["pp"] += 1
                    P.mm(ps[pp], permT, qb[qi], True, True, ["permT", "qb%d" % qi], ["ps%d" % pp])
                    continue

                def second(qi=qi, typ=typ, g=g, idx=idx, t=t):
                    pp = 3 + cnt["pp"] % 2
                    cnt["pp"] += 1
                    P.mm(ps[pp], permT, qb[qi], True, True, ["permT", "qb%d" % qi], ["ps%d" % pp])
                    P.tt("dve", t2[qi], ps[pp], ropeS[:, T * t:T * t + T], ALU.mult,
                         ["ps%d" % pp, "ropeS"], ["t2_%d" % qi])
                    sb = cnt["st"] % 3
                    cnt["st"] += 1
                    need32 = (kind == 2)
                    if need32:
                        P.tt("dve", qr[qi], t1[qi], t2[qi], ALU.add,
                             ["t1_%d" % qi, "t2_%d" % qi], ["qr%d" % qi])
                        P.cp("act", stb[sb], qr[qi], ["qr%d" % qi], ["stb%d" % sb])
                    else:
                        P.tt("dve", stb[sb], t1[qi], t2[qi], ALU.add,
                             ["t1_%d" % qi, "t2_%d" % qi], ["stb%d" % sb])
                    if typ == "Q":
                        P.dma("sp", qs[g, 128 * idx:128 * idx + 128, T * t:T * t + T], stb[sb],
                              ["stb%d" % sb], ["qs"])
                        if need32:
                            P.dma("sp", q32[128 * idx:128 * idx + 128, T * t:T * t + T], qr[qi],
                                  ["qr%d" % qi], ["q32"])
                    else:
                        P.dma("sp", ks[g, 128 * idx:128 * idx + 128, T * t:T * t + T], stb[sb],
                              ["stb%d" % sb], ["ks"])
                        if need32:
                            P.reduce_add("dve", ksum[:, 2 * t:2 * t + 2],
                                         qr[qi].rearrange("p (b k) -> p b k", b=2),
                                         ["qr%d" % qi], ["ksum"])

                pending.append(second)
            flush_pending()
            if typ == "K" and kind == 2:
                P.dma("sp", ksum_d[128 * idx:128 * idx + 128, :], ksum[:, 0:16], ["ksum"], ["ksum_d"])
        P.barrier()
        if stop and stop[0] == 'P' and li == NL - 1:
            which = stop[1:]
            src = {'q0': qs[0], 'q1': qs[1], 'q2': qs[2], 'z': zs}.get(which)
            if which == 'XH':
                for c in range(8):
                    P.dma("pool", outT[128 * c:128 * c + 128, :], XH[:, c, :], [], ["outT"])
            elif src is not None:
                for c in range(8):
                    P.dma("pool", outT[128 * c:128 * c + 128, :], src[128 * c:128 * c + 128, :], [], ["outT"])
            elif which[0] == 'k':
                g = int(which[1])
                for c in range(2):
                    P.dma("pool", outT[128 * c:128 * c + 128, :], ks[g, 128 * c:128 * c + 128, :], [], ["outT"])
            elif which[0] == 'v':
                g = int(which[1])
                for c in range(8):
                    P.dma("pool", outT[0:256, 512 * c:512 * c + 512].rearrange("c s -> s c"),
                          vs[g, 512 * c:512 * c + 512, :], [], ["outT"])
            return _finish(nc, P)

        A.reset()
        if os.environ.get('K_PAD'):
            A.alloc(int(os.environ['K_PAD']))
        yst = [A.alloc(512) for _ in range(2)]
        SZ = [A.alloc(512) for _ in range(3)]
        loads_list = []
        _tight = kind == 1 or os.environ.get('K_TIGHT') == '1'
        NPT = 3 if _tight else 4
        NQB = 2 if _tight else 3
        PT = [A.alloc(512) for _ in range(NPT)]
        rd = A.alloc(512, F32)
        gg = rd
        initrow = A.alloc(16 * 128 if kind == 0 else 128)
        onesrow = A.alloc(512)
        P.memset("dve", initrow[0:1, :], 0.0, ["initrow"])
        P.memset("dve", onesrow[0:1, :], 1.0, ["onesrow"])
        if kind == 0:
            sk = A.alloc(16, F32)
            P.dma("sp", sk[0:1, :], sink[li], [], ["sk"])
            for h in range(16):
                isB = h >= 8
                c0 = 128 * h + (0 if isB else 64)
                P.act(initrow[0:1, c0:c0 + 64], sk[0:1, h:h + 1].to_broadcast([1, 64]), AF.Exp,
                      ["sk", "initrow"], ["initrow"])

        VW = 32 * 192

        def load_v(g, kp, Vt, dil, tag):
            V3 = Vt.rearrange("p (b c) -> p b c", c=192)
            P.memset("pool", V3[:, :, 64:128], 1.0, ["Vt" + tag])
            nb = 32 // dil
            for r in range(dil):
                for half in range(2):
                    c0 = 128 * kp + 64 * half
                    src = vs[g].rearrange("(b p r) c -> r p b c", p=128, r=dil)[r][:, :, c0:c0 + 64]
                    dst = V3[:, nb * r:nb * (r + 1), 128 * half:128 * half + 64]
                    P.dma("sp", dst, src, ["vs"], ["Vt" + tag])

        def load_kv(g, kp, KT, Vt, dil, tag):
            P.dma("sp", KT, ks[g, 128 * kp:128 * kp + 128, :], ["ks"], ["KT" + tag])
            load_v(g, kp, Vt, dil, tag)

        def vaug(Vt, kt, isB):
            base = 192 * kt + (64 if isB else 0)
            return Vt[:, base:base + 128]

        items = []
        st = {"s": 0, "pt": 0, "o": 0, "ld": 0}

        def epilogue(pi, t, oA, oB, szb):
            P.recip("dve", rd[0:64, :], ps[oA][64:128, :], ["ps%d" % oA], ["gg"])
            P.recip("dve", rd[64:128, :], ps[oB][0:64, :], ["ps%d" % oB], ["gg"])
            P.tt("pool", gg, rd, SZ[szb], ALU.mult, ["SZ%d" % szb, "gg"], ["gg"])
            yb = st["ld"] % 2
            if os.environ.get('K_DEN') == '1':
                P.cp("dve", yst[yb][0:64, :], ps[oA][64:128, :], ["ps%d" % oA], ["yst%d" % yb])
                P.cp("dve", yst[yb][64:128, :], ps[oB][0:64, :], ["ps%d" % oB], ["yst%d" % yb])
                P.dma("sp", ys[128 * pi:128 * pi + 128, T * t:T * t + T], yst[yb], ["yst%d" % yb], ["ys"])
                st["ld"] += 1
                return
            P.tt("dve", yst[yb][0:64, :], ps[oA][0:64, :], gg[0:64, :], ALU.mult,
                 ["ps%d" % oA, "gg"], ["yst%d" % yb])
            P.tt("dve", yst[yb][64:128, :], ps[oB][64:128, :], gg[64:128, :], ALU.mult,
                 ["ps%d" % oB, "gg"], ["yst%d" % yb])
            P.dma("sp", ys[128 * pi:128 * pi + 128, T * t:T * t + T], yst[yb], ["yst%d" % yb], ["ys"])
            st["ld"] += 1

        if kind in (0, 1):
            mask4 = A.alloc(512)
            P.dma("pool", mask4, mswa_d if kind == 0 else mdil_d, [], ["mask4"])
            KTs = [A.alloc(S) for _ in range(ng)]
            Vts = [A.alloc(VW) for _ in range(ng)]
            QTs = [[A.alloc(512) for _ in range(NQB)] for _ in range(ng)]
            cur_kp = -1
            for pi, (ha, hb, kp) in enumerate(pairs):
                newkp = kp != cur_kp
                cur_kp = kp

                def kvload(kp=kp):
                    for g in range(ng):
                        if kind == 1 and str(g) not in os.environ.get('K_GRP', '012') + os.environ.get('K_LDG', ''):
                            continue
                        load_kv(g, kp, KTs[g], Vts[g], DIL[g][1] if kind == 1 else 1, str(g))
                for t in range(NT):
                    ti = pi * NT + t
                    lb = ti % 3
                    lq = ti % NQB
                    flush_before = newkp and t == 0

                    def loads(pi=pi, t=t, lb=lb, lq=lq):
                        for g in range(ng):
                            if kind == 1 and str(g) not in os.environ.get('K_GRP', '012') + os.environ.get('K_LDQ', ''):
                                continue
                            P.dma("sp", QTs[g][lq], qs[g, 128 * pi:128 * pi + 128, T * t:T * t + T],
                                  ["qs"], ["QT%d_%d" % (g, lq)])
                        P.dma("sp", SZ[lb], zs[128 * pi:128 * pi + 128, T * t:T * t + T],
                              ["zs"], ["SZ%d" % lb])

                    loads_list.append(loads)
                    obanks = []
                    first_item_of_tile = True
                    for xi, h in enumerate((ha, hb)):
                        isB = xi == 1
                        R = slice(64, 128) if isB else slice(0, 64)
                        ob = 4 + (st["o"] % 4)
                        st["o"] += 1
                        obanks.append(ob)
                        banks = []
                        for g in range(ng):
                            if kind == 1 and str(g) not in os.environ.get('K_GRP', '012'):
                                continue
                            dil = DIL[g][1] if kind == 1 else 1
                            nq = T // dil
                            slots = []
                            if nq >= 128:
                                nb = 32 // dil
                                for r in range(dil):
                                    for bq in range(nq // 128):
                                        fb = (T * t // dil) // 128 + bq
                                        qsl = slice(r + dil * 128 * bq, r + dil * 128 * bq + dil * 127 + 1, dil) if dil > 1 \
                                            else slice(128 * bq, 128 * bq + 128)
                                        for which, kb in ((0, fb - 1), (1, fb)):
                                            valid = kb >= 0
                                            kbb = kb if valid else fb
                                            ksl = slice(r + dil * 128 * kbb, r + dil * 128 * kbb + dil * 127 + 1, dil) if dil > 1 \
                                                else slice(128 * kbb, 128 * kbb + 128)
                                            slots.append((g, ksl, qsl, r * nb + kbb, valid, which, 128, 0))
                            else:
                                bb = (T * t) // 2048
                                u = ((T * t) % 2048) // T
                                nb = 2
                                for r in range(dil):
                                    qsl = slice(r, r + dil * (nq - 1) + 1, dil)
                                    for which, kb in ((0, bb - 1), (1, bb)):
                                        valid = kb >= 0
                                        kbb = kb if valid else bb
                                        ksl = slice(r + 2048 * kbb, r + 2048 * kbb + dil * 127 + 1, dil)
                                        slots.append((g, ksl, qsl, r * nb + kbb, valid, which, nq, nq * u))
                            cur, used = [], 0
                            for sl in slots:
                                if used + sl[6] > 512:
                                    banks.append(cur)
                                    cur, used = [], 0
                                cur.append(sl + (used,))
                                used += sl[6]
                            if cur:
                                banks.append(cur)
                        for bi, bank in enumerate(banks):
                            def qk(bank=bank, R=R, lb=lq, first=(bi == 0), ob=ob, h=h, do_loads=first_item_of_tile,
                                   ti=ti, kvl=(kvload if (flush_before and first_item_of_tile) else None),
                                   _pi=pi, _xi=xi, _t=t):
                                if kvl is not None:
                                    kvl()
                                if do_loads:
                                    if ti == 0:
                                        loads_list[0]()
                                    if ti + 1 < len(loads_list):
                                        loads_list[ti + 1]()
                                if os.environ.get('K_PTD') == '1' and _pi == 0 and _xi == 0 and first:
                                    P.dma("pool", q32[256:384, T * _t:T * _t + T], QTs[0][lb],
                                          ["QT0_%d" % lb], ["q32"])
                                    if _t == 0:
                                        P.dma("pool", q32[384:512, :], KTs[0], ["KT0"], ["q32"])
                                sbk = st["s"] % 3
                                st["s"] += 1
                                if first:
                                    ih = h if kind == 0 else 0
                                    P.mm(ps[ob], initrow[0:1, 128 * ih:128 * ih + 128], onesrow[0:1, :], True, False,
                                         ["initrow", "onesrow"], ["ps%d" % ob])
                                for (g, ksl, qsl, kt, valid, which, w, moff, col) in bank:
                                    P.mm(ps[sbk][:, col:col + w], KTs[g][R, ksl], QTs[g][lb][R, qsl], True, True,
                                         ["KT%d" % g, "QT%d_%d" % (g, lb)], ["ps%d" % sbk])
                                return sbk

                            def sm(sbk, bank=bank, _pi=pi, _xi=xi, _bi=bi, _t=t):
                                pt = st["pt"] % NPT
                                st["pt"] += 1
                                ncol = sum(sl[6] for sl in bank)
                                P.act(PT[pt][:, 0:ncol], ps[sbk][:, 0:ncol], AF.Exp, ["ps%d" % sbk], ["PT%d" % pt],
                                      scale=SCALE)
                                w = bank[0][6]
                                if w == 128:
                                    P.tt("dve", PT[pt][:, 0:ncol], PT[pt][:, 0:ncol], mask4[:, 0:ncol], ALU.mult,
                                         ["PT%d" % pt, "mask4"], ["PT%d" % pt])
                                else:
                                    moff = bank[0][7]
                                    nsl = len(bank) // 2
                                    mv = mask4[:, 0:256].rearrange("p (a c) -> p a c", a=2)[:, :, moff:moff + w]
                                    for half in range(0, nsl, 1):
                                        pass
                                    pv_ = PT[pt][:, 0:ncol].rearrange("p (s a c) -> p s a c", a=2, c=w)
                                    P.op("dve", lambda e, pv_=pv_, mv=mv, nsl=nsl, w=w: e.tensor_tensor(
                                        pv_, pv_, mv.unsqueeze(1).to_broadcast([128, nsl, 2, w]), ALU.mult),
                                        ["PT%d" % pt, "mask4"], ["PT%d" % pt])
                                if os.environ.get('K_PTD') == '1' and _pi == 0 and _xi == 0:
                                    P.dma("pool", q32[128 * _bi:128 * _bi + 128, T * _t:T * _t + T], PT[pt],
                                          ["PT%d" % pt], ["q32"])
                                return pt

                            def pv(pt, bank=bank, isB=isB, ob=ob, last=(bi == len(banks) - 1)):
                                vb = [sl for sl in bank if sl[4]]
                                for i, (g, ksl, qsl, kt, valid, which, w, moff, col) in enumerate(vb):
                                    P.mm(ps[ob][:, qsl], vaug(Vts[g], kt, isB), PT[pt][:, col:col + w], False,
                                         last and i == len(vb) - 1,
                                         ["Vt%d" % g, "PT%d" % pt], ["ps%d" % ob])

                            items.append([qk, sm, pv, None, flush_before and first_item_of_tile])
                            first_item_of_tile = False
                    oA, oB = obanks
                    items[-1][3] = (lambda pi=pi, t=t, oA=oA, oB=oB, lb=lb: epilogue(pi, t, oA, oB, lb))
        else:
            mown = A.alloc(4 * 512)
            P.dma("pool", mown, mown_d, [], ["mown"])
            ident = A.alloc(128)
            P.dma("pool", ident, ident_d, [], ["ident"])
            KX = [A.alloc(S) for _ in range(2)]
            Vt = A.alloc(VW)
            QX = [[A.alloc(512) for _ in range(3)] for _ in range(2)]
            Q32 = [[A.alloc(512, F32) for _ in range(3)] for _ in range(2)]
            ksA = [A.alloc(16, F32) for _ in range(2)]
            gsb = A.alloc(64, F32)
            top8 = A.alloc(32, F32)
            selb = A.alloc(64, F32)
            biasp = A.alloc(4 * 80)
            P.memset("dve", biasp, 0.0, ["biasp"])
            psT = ps[3].bitcast(BF16)
            cur_kp = -1
            for pi, (ha, hb, kp) in enumerate(pairs):
                newkp = kp != cur_kp
                cur_kp = kp

                def kvload(kp=kp):
                    load_v(0, kp, Vt, 1, "m")
                    for xi in range(2):
                        P.dma("sp", KX[xi][0:64, :], ks[0, 128 * kp + 64 * xi:128 * kp + 64 * xi + 64, :],
                              ["ks"], ["KX%d" % xi])
                        P.dma("pool", KX[xi][64:80, :], erow_d, [], ["KX%d" % xi])
                        P.dma("sp", ksA[xi][0:64, :], ksum_d[128 * kp + 64 * xi:128 * kp + 64 * xi + 64, :],
                              ["ksum_d"], ["ksA%d" % xi])
                for t in range(NT):
                    ti = pi * NT + t
                    lb = ti % 3
                    flush_before = newkp and t == 0

                    def loads(pi=pi, t=t, lb=lb):
                        for xi in range(2):
                            r0 = 128 * pi + 64 * xi
                            P.dma("sp", QX[xi][lb][0:64, :], qs[0, r0:r0 + 64, T * t:T * t + T],
                                  ["qs"], ["QX%d_%d" % (xi, lb)])
                            P.dma("sp", Q32[xi][lb][0:64, :], q32[r0:r0 + 64, T * t:T * t + T],
                                  ["q32"], ["Q32%d_%d" % (xi, lb)])
                        P.dma("sp", SZ[lb], zs[128 * pi:128 * pi + 128, T * t:T * t + T],
                              ["zs"], ["SZ%d" % lb])

                    loads_list.append(loads)

                    def gate(xi, t=t, lb=lb):
                        gp = 2
                        for j in range(4):
                            P.mm(ps[gp][:, 16 * j:16 * j + 16], Q32[xi][lb][0:64, 128 * j:128 * j + 128],
                                 ksA[xi][0:64, :], True, True,
                                 ["Q32%d_%d" % (xi, lb), "ksA%d" % xi], ["ps%d" % gp])
                        g3 = gsb.rearrange("p (j n) -> p j n", j=4)
                        s3 = selb.rearrange("p (j n) -> p j n", j=4)
                        t3 = top8.rearrange("p (j n) -> p j n", j=4)
                        b3 = biasp.rearrange("p (j n) -> p j n", j=4)
                        P.memset("dve", gsb, -1e30, ["gsb"])
                        for j in range(4):
                            own = 2 * t + j // 2
                            if own > 0:
                                P.cp("dve", g3[:, j, 0:own], ps[gp][:, 16 * j:16 * j + own], ["ps%d" % gp], ["gsb"])
                        P.memset("dve", selb, -BIG, ["selb"])
                        for j in range(4):
                            own = 2 * t + j // 2
                            if own > 3:
                                P.max8("dve", t3[:, j, :], g3[:, j, :], ["gsb"], ["top8"])
                                P.ts("dve", s3[:, j, 0:own], g3[:, j, 0:own], t3[:, j, 2:3], None, ALU.is_ge, None,
                                     ["gsb", "top8", "selb"], ["selb"])
                                P.ts("dve", s3[:, j, 0:own], s3[:, j, 0:own], BIG, -BIG, ALU.mult, ALU.add,
                                     ["selb"], ["selb"])
                                P.memset("dve", s3[:, j, own:own + 1], 0.0, ["selb"])
                            else:
                                P.memset("dve", s3[:, j, 0:own + 1], 0.0, ["selb"])
                        P.cp("dve", b3[:, :, 64:80], s3, ["selb", "biasp"], ["biasp"])
                        for j in range(4):
                            P.tr(psT[0:80, 128 * j:128 * j + 128], b3[:, j, :], ident, ["biasp", "ident"], ["ps3"])
                        P.cp("dve", QX[xi][lb][64:80, :], psT[64:80, 0:512], ["ps3"], ["QX%d_%d" % (xi, lb)])

                    obanks = []
                    first_item_of_tile = True
                    for xi, h in enumerate((ha, hb)):
                        isB = xi == 1
                        ob = 4 + (st["o"] % 4)
                        st["o"] += 1
                        obanks.append(ob)
                        nkt = 4 * (t + 1)
                        for kt in range(nkt):
                            def qk(kt=kt, xi=xi, lb=lb, ob=ob, h=h, do_loads=first_item_of_tile, gate=gate,
                                   ti=ti, kvl=(kvload if (flush_before and first_item_of_tile) else None)):
                                if kvl is not None:
                                    kvl()
                                if do_loads:
                                    if ti == 0:
                                        loads_list[0]()
                                    if ti + 1 < len(loads_list):
                                        loads_list[ti + 1]()
                                if kt == 0:
                                    gate(xi)
                                    P.mm(ps[ob], initrow[0:1, 0:128], onesrow[0:1, :], True, False,
                                         ["initrow", "onesrow"], ["ps%d" % ob])
                                sbk = st["s"] % 2
                                st["s"] += 1
                                P.mm(ps[sbk], KX[xi][0:80, 128 * kt:128 * kt + 128], QX[xi][lb][0:80, :], True, True,
                                     ["KX%d" % xi, "QX%d_%d" % (xi, lb)], ["ps%d" % sbk])
                                return sbk

                            def sm(sbk, kt=kt, t=t):
                                pt = st["pt"] % 4
                                st["pt"] += 1
                                P.act(PT[pt], ps[sbk], AF.Exp, ["ps%d" % sbk], ["PT%d" % pt], scale=SCALE)
                                if kt >= 4 * t:
                                    m = mown[:, 512 * (kt - 4 * t):512 * (kt - 4 * t) + 512]
                                    P.tt("pool", PT[pt], PT[pt], m, ALU.mult, ["PT%d" % pt, "mown"], ["PT%d" % pt])
                                return pt

                            def pv(pt, kt=kt, isB=isB, ob=ob, last=(kt == nkt - 1)):
                                P.mm(ps[ob], vaug(Vt, kt, isB), PT[pt], False, last,
                                     ["Vtm", "PT%d" % pt], ["ps%d" % ob])

                            items.append([qk, sm, pv, None, flush_before and first_item_of_tile])
                            first_item_of_tile = False
                    oA, oB = obanks
                    items[-1][3] = (lambda pi=pi, t=t, oA=oA, oB=oB, lb=lb: epilogue(pi, t, oA, oB, lb))

        if _skip:
            items = []
        prev = None
        for it in items:
            if it[4] and prev is not None:
                prev[0][2](prev[1])
                if prev[0][3] is not None:
                    prev[0][3]()
                prev = None
            sbk = it[0]()
            if prev is not None:
                prev[0][2](prev[1])
                if prev[0][3] is not None:
                    prev[0][3]()
            pt = it[1](sbk)
            prev = (it, pt)
        if prev is not None:
            prev[0][2](prev[1])
            if prev[0][3] is not None:
                prev[0][3]()
        P.barrier()
        if stop == 'A' and li == NL - 1:
            for c in range(8):
                if os.environ.get('K_XHD') == '1':
                    P.dma("pool", outT[128 * c:128 * c + 128, :], XH[:, c, :], [], ["outT"])
                elif os.environ.get('K_PTD') == '1':
                    P.dma("sp", outT[128 * c:128 * c + 128, :], q32[128 * c:128 * c + 128, :], [], ["outT"])
                else:
                    P.dma("pool", outT[128 * c:128 * c + 128, :], ys[128 * c:128 * c + 128, :], [], ["outT"])
            return _finish(nc, P)

        A.reset()
        Yb = [A.alloc(8 * 512) for _ in range(2)]
        Wo = A.alloc(8 * 1024)
        T32 = [A.alloc(8 * 512, F32) for _ in range(2)]
        SQ = [A.alloc(512, F32) for _ in range(2)]
        mean_sb = A.alloc(512, F32)
        rstd_sb = A.alloc(512, F32)
        m2 = A.alloc(512, F32)
        onesM = A.alloc(128, F32)
        G = A.alloc(8, F32)
        Bt = A.alloc(8, F32)
        P.memset("dve", onesM, 1.0 / D, ["onesM"])
        P.dma("sp", G, lng[li], [], ["G"])
        P.dma("sp", Bt, lnb[li], [], ["Bt"])
        Wo3 = Wo.rearrange("p (c n) -> p c n", c=8)
        for pi, (ha, hb, kp) in enumerate(pairs):
            P.dma("pool", Wo3[0:64, pi, :], w_out[li][64 * ha:64 * ha + 64, :], [], ["Wo"])
            P.dma("pool", Wo3[64:128, pi, :], w_out[li][64 * hb:64 * hb + 64, :], [], ["Wo"])
        last_layer = li == NL - 1
        for t in range(NT):
            yb = t % 2
            Y3 = Yb[yb].rearrange("p (c n) -> p c n", c=8)
            P.dma("sp", Y3, ys.rearrange("(c p) s -> p c s", p=128)[:, :, T * t:T * t + T], ["ys"], ["Y%d" % yb])
            T3 = T32[yb].rearrange("p (c n) -> p c n", c=8)
            tr_ = "T32_%d" % yb
            for j in range(8):
                pb = j % 3
                for c in range(8):
                    P.mm(ps[pb], Wo3[:, c, 128 * j:128 * j + 128], Y3[:, c, :], c == 0, c == 7,
                         ["Wo", "Y%d" % yb], ["ps%d" % pb])
                P.stt("dve", T3[:, j, :], XH[:, j, T * t:T * t + T], ALPHA, ps[pb], ALU.mult, ALU.add,
                      ["ps%d" % pb, xres(j, t) + "h"], [tr_ + "_%d" % j])
                P.stt("dve", T3[:, j, :], XL[:, j, T * t:T * t + T], ALPHA, T3[:, j, :], ALU.mult, ALU.add,
                      [tr_ + "_%d" % j, xres(j, t) + "l"], [tr_ + "_%d" % j])
            _ol = int(os.environ.get('K_OLVL', '9'))
            if _ol == 1:
                continue
            for j in range(8):
                P.mm(ps[3], onesM, T3[:, j, :], j == 0, j == 7, ["onesM", tr_ + "_%d" % j], ["ps3"])
            for j in range(8):
                sq = j % 2
                P.act(SQ[sq], T3[:, j, :], AF.Square, [tr_ + "_%d" % j], ["SQ%d" % sq])
                P.mm(ps[4], onesM, SQ[sq], j == 0, j == 7, ["onesM", "SQ%d" % sq], ["ps4"])
            if _ol == 2:
                continue
            P.cp("act", mean_sb, ps[3], ["ps3"], ["mean"])
            P.tt("dve", m2, mean_sb, mean_sb, ALU.mult, ["mean"], ["m2"])
            P.tt("dve", m2, ps[4], m2, ALU.subtract, ["ps4", "m2"], ["m2"])
            P.ts("dve", m2, m2, EPS, None, ALU.add, None, ["m2"], ["m2"])
            P.act(rstd_sb, m2, AF.Sqrt, ["m2"], ["rstd"])
            P.recip("dve", rstd_sb, rstd_sb, ["rstd"], ["rstd"])
            if _ol == 3:
                continue
            for j in range(8):
                rn = tr_ + "_%d" % j
                P.tt("dve", T3[:, j, :], T3[:, j, :], mean_sb, ALU.subtract, [rn, "mean"], [rn])
                P.tt("dve", T3[:, j, :], T3[:, j, :], rstd_sb, ALU.mult, [rn, "rstd"], [rn])
                P.ts("dve", T3[:, j, :], T3[:, j, :], G[:, j:j + 1], Bt[:, j:j + 1], ALU.mult, ALU.add,
                     [rn, "G", "Bt"], [rn])
                if last_layer:
                    P.dma("sp", outT[128 * j:128 * j + 128, T * t:T * t + T], T3[:, j, :], [rn], ["outT"])
                else:
                    P.cp("act", XH[:, j, T * t:T * t + T], T3[:, j, :], [rn], [xres(j, t) + "h"])
                    P.tt("dve", XL[:, j, T * t:T * t + T], T3[:, j, :], XH[:, j, T * t:T * t + T], ALU.subtract,
                         [rn, xres(j, t) + "h"], [xres(j, t) + "l"])
        P.barrier()
        if stop == 'O%d' % li:
            for j in range(8):
                src = XL if os.environ.get('K_XL') == '1' else XH
                P.dma("pool", outT[128 * j:128 * j + 128, :], src[:, j, :], [], ["outT"])
            return _finish(nc, P)

    return _finish(nc, P)


def _finish(nc, P):
    with nc.allow_low_precision("bf16 matmul operands, fp32 accumulation"):
        with nc.allow_non_contiguous_dma("layout"):
            P.emit()
    return nc


def _consts():
    pos = np.arange(S, dtype=np.float32)
    inv = (np.float32(500000.0) ** (-np.arange(0, 16, 2, dtype=np.float32) / np.float32(16))).astype(np.float32)
    ang = pos[None, :] * inv[:, None]
    cos = np.cos(ang).astype(np.float32)
    sin = np.sin(ang).astype(np.float32)
    C = np.ones((128, S), np.float32)
    Sn = np.zeros((128, S), np.float32)
    permT = np.zeros((128, 128), np.float32)
    for base in (0, 64):
        C[base:base + 8] = cos
        C[base + 8:base + 16] = cos
        Sn[base:base + 8] = sin
        Sn[base + 8:base + 16] = sin
        for i in range(8):
            permT[base + i + 8, base + i] = -1.0
            permT[base + i, base + i + 8] = 1.0
    j = np.arange(128)[:, None]
    r = np.arange(128)[None, :]
    cur = (j <= r).astype(np.float32)
    prev_swa = (j >= r + 1).astype(np.float32)
    prev_dil = (j >= r).astype(np.float32)
    mswa = np.concatenate([prev_swa, cur, prev_swa, cur], axis=1)
    mdil = np.concatenate([prev_dil, cur, prev_dil, cur], axis=1)
    one = np.ones((128, 128), np.float32)
    zero = np.zeros((128, 128), np.float32)
    mown = np.concatenate([
        cur, one, one, one,
        zero, cur, one, one,
        zero, zero, cur, one,
        zero, zero, zero, cur], axis=1)
    erows = np.zeros((16, S), np.float32)
    for n in range(16):
        erows[n, 256 * n:256 * n + 256] = 1.0
    ident = np.eye(128, dtype=np.float32)
    return dict(ropeC=C, ropeS=Sn, permT=permT, mask_swa=mswa, mask_dil=mdil, mask_own=mown,
                erows=erows, ident=ident)


_NC_CACHE = {}


def make_in_maps(inputs, NL, cores):
    consts = _consts()
    maps = []
    for c in cores:
        m = dict(consts)
        m["xT"] = np.ascontiguousarray(np.asarray(inputs["x"][c], dtype=np.float32).T)
        for i in range(NL):
            m["w_in_%d" % i] = np.ascontiguousarray(inputs["w_in_%d" % i], dtype=np.float32)
            m["w_out_%d" % i] = np.ascontiguousarray(inputs["w_out_%d" % i], dtype=np.float32)
            m["lng_%d" % i] = np.ascontiguousarray(np.asarray(inputs["ln_g_%d" % i], np.float32).reshape(8, 128).T)
            m["lnb_%d" % i] = np.ascontiguousarray(np.asarray(inputs["ln_b_%d" % i], np.float32).reshape(8, 128).T)
            if i % 3 == 0:
                m["sink_%d" % i] = np.asarray(inputs["sink_%d" % i], np.float32).reshape(1, 16)
        maps.append(m)
    return maps


def kernel(x, w_in_0, sink_0, w_out_0, ln_g_0, ln_b_0,
           w_in_1, w_out_1, ln_g_1, ln_b_1,
           w_in_2, w_out_2, ln_g_2, ln_b_2,
           w_in_3, sink_3, w_out_3, ln_g_3, ln_b_3):
    inputs = dict(x=x, w_in_0=w_in_0, sink_0=sink_0, w_out_0=w_out_0, ln_g_0=ln_g_0, ln_b_0=ln_b_0,
                  w_in_1=w_in_1, w_out_1=w_out_1, ln_g_1=ln_g_1, ln_b_1=ln_b_1,
                  w_in_2=w_in_2, w_out_2=w_out_2, ln_g_2=ln_g_2, ln_b_2=ln_b_2,
                  w_in_3=w_in_3, sink_3=sink_3, w_out_3=w_out_3, ln_g_3=ln_g_3, ln_b_3=ln_b_3)
    NL = 4
    if NL not in _NC_CACHE:
        _NC_CACHE[NL] = build_program(NL)
    nc = _NC_CACHE[NL]
    cores = list(range(8))
    in_maps = make_in_maps(inputs, NL, cores)
    res = run_bass_kernel_spmd(nc, in_maps, core_ids=cores)
    out = np.stack([np.asarray(r["outT"], dtype=np.float32).T for r in res.results], axis=0)
    return np.ascontiguousarray(out)
```

```python
import numpy as np
import concourse.bass as bass
import concourse.mybir as mybir
from concourse.ap import AP
from concourse.bass_utils import run_bass_kernel_spmd

F32 = mybir.dt.float32
BF16 = mybir.dt.bfloat16
AF = mybir.ActivationFunctionType
ALU = mybir.AluOpType
AX = mybir.AxisListType

S = 4096
D = 1024
T = 512
NT = S // T
ALPHA = 8.0 ** 0.25
EPS = 1e-5
SCALE = 0.125
BIG = 30000.0
DIL = ((128, 1), (512, 4), (2048, 16))


class Prog:
    def __init__(self, nc):
        self.nc = nc
        self.engs = ["pe", "act", "dve", "pool", "sp"]
        self.streams = {e: [] for e in self.engs}
        self.count = {e: 0 for e in self.engs}
        self.semh = {}
        self.dma_n = {e: 0 for e in self.engs}
        self.NDS = 16
        self.waited = {e: {} for e in self.engs}
        self.lastw = {}
        self.readers = {}

    def sem(self, key):
        if key not in self.semh:
            self.semh[key] = self.nc.alloc_semaphore(key)
        return self.semh[key]

    def op(self, eng, fn, reads=(), writes=(), dma=False):
        writes = list(writes) + [r for r in reads if r.startswith("ps") and r not in writes]
        reads = [r for r in reads if not r.startswith("ps")]
        deps = {}

        def add(tok):
            if tok is None:
                return
            k, v, e = tok
            if e == eng and eng == "pe" and k == "c_pe":
                return
            if deps.get(k, 0) < v:
                deps[k] = v

        for r in reads:
            add(self.lastw.get(r))
        for w in writes:
            add(self.lastw.get(w))
            for tok in self.readers.get(w, {}).values():
                add(tok)
        if dma:
            i = self.dma_n[eng]
            self.dma_n[eng] += 1
            k = "d_%s_%d" % (eng, i % self.NDS)
            val = 16 * (i // self.NDS + 1)
            if val > 16:
                if deps.get(k, 0) < val - 16:
                    deps[k] = val - 16
            tok = (k, val, eng)
        else:
            self.count[eng] += 1
            tok = ("c_" + eng, self.count[eng], eng)
        waits = []
        for k, v in deps.items():
            if self.waited[eng].get(k, 0) < v:
                self.waited[eng][k] = v
                waits.append((k, v))
        self.streams[eng].append((waits, fn, tok[0], 16 if dma else 1))
        for r in reads:
            d = self.readers.setdefault(r, {})
            if tok[0] not in d or d[tok[0]][1] < tok[1]:
                d[tok[0]] = tok
        for w in writes:
            self.lastw[w] = tok
            self.readers[w] = {}
        return tok

    def mm(self, out, lhsT, rhs, start, stop, reads, writes):
        self.op("pe", lambda e: e.matmul(out, lhsT, rhs, start=start, stop=stop,
                                         skip_group_check=True), reads, writes)

    def tr(self, out, in_, ident, reads, writes):
        self.op("pe", lambda e: e.transpose(out, in_, ident), reads, writes)

    def act(self, out, in_, func, reads, writes, **kw):
        self.op("act", lambda e: e.activation(out, in_, func, **kw), reads, writes)

    def tt(self, eng, out, in0, in1, op, reads, writes):
        self.op(eng, lambda e: e.tensor_tensor(out, in0, in1, op), reads, writes)

    def ts(self, eng, out, in0, s1, s2, op0, op1, reads, writes):
        if s2 is None:
            self.op(eng, lambda e: e.tensor_scalar(out, in0, s1, None, op0), reads, writes)
        else:
            self.op(eng, lambda e: e.tensor_scalar(out, in0, s1, s2, op0, op1), reads, writes)

    def stt(self, eng, out, in0, scalar, in1, op0, op1, reads, writes):
        self.op(eng, lambda e: e.scalar_tensor_tensor(out, in0, scalar, in1, op0, op1),
                reads, writes)

    def cp(self, eng, out, in_, reads, writes):
        if eng == "act":
            self.op(eng, lambda e: e.activation(out, in_, AF.Copy), reads, writes)
        else:
            self.op(eng, lambda e: e.tensor_copy(out, in_), reads, writes)

    def recip(self, eng, out, in_, reads, writes):
        self.op(eng, lambda e: e.reciprocal(out, in_), reads, writes)

    def reduce_add(self, eng, out, in_, reads, writes):
        self.op(eng, lambda e: e.tensor_reduce(out, in_, AX.X, ALU.add), reads, writes)

    def max8(self, eng, out, in_, reads, writes):
        self.op(eng, lambda e: e.max(out, in_), reads, writes)

    def memset(self, eng, ap, val, writes):
        self.op(eng, lambda e: e.memset(ap, val), (), writes)

    def dma(self, q, out, in_, reads, writes):
        self.op(q, lambda e: e.dma_start(out=out, in_=in_), reads, writes, dma=True)

    def barrier(self):
        allres = list(self.lastw.keys() | self.readers.keys())
        for e in ["pe", "act", "dve", "pool", "sp"]:
            pass
        self._bar = getattr(self, "_bar", 0) + 1
        toks = {}
        for r in allres:
            t = self.lastw.get(r)
            if t is not None and toks.get(t[0], (0,))[0] < t[1]:
                toks[t[0]] = (t[1], t[2])
            for t in self.readers.get(r, {}).values():
                if toks.get(t[0], (0,))[0] < t[1]:
                    toks[t[0]] = (t[1], t[2])
        for e in self.engs:
            waits = []
            for k, (v, te) in toks.items():
                if te == e and e == "pe" and k == "c_pe":
                    continue
                if self.waited[e].get(k, 0) < v:
                    self.waited[e][k] = v
                    waits.append((k, v))
            if waits:
                self.streams[e].append((waits, None, None, 0))
        self.lastw = {}
        self.readers = {}

    def emit(self):
        nc = self.nc
        self.barrier()
        emap = {"pe": "tensor", "act": "scalar", "dve": "vector", "pool": "gpsimd", "sp": "sync"}
        with nc.Block() as block:
            for eng in self.engs:
                stream = self.streams[eng]

                def body(e, stream=stream):
                    for waits, fn, inc, amt in stream:
                        for k, v in waits:
                            e.wait_ge(self.sem(k), v)
                        if fn is not None:
                            fn(e).then_inc(self.sem(inc), amt)

                getattr(block, emap[eng])(body)


class Arena:
    def __init__(self, t, ncols):
        self.t = t
        self.n = ncols
        self.off = 0

    def reset(self):
        self.off = 0

    def alloc(self, cols, dtype=BF16):
        w = cols * (2 if dtype == F32 else 1)
        self.off = (self.off + 15) // 16 * 16
        a = self.off
        self.off += w
        assert self.off <= self.n, ("arena overflow", self.off, self.n)
        ap = self.t[:, a:a + w]
        if dtype == F32:
            ap = ap.bitcast(F32)
        return ap


def layer_cols(kind):
    if kind == 0:
        return [(0, 1024, 1152)], 1280, 2
    if kind == 1:
        return [(g * 1536, g * 1536 + 1024, g * 1536 + 1280) for g in range(3)], 4608, 4
    return [(0, 1024, 1280)], 1536, 4


def head_pairs(kind):
    if kind == 0:
        return [(h, h + 8, 0) for h in range(8)]
    return [(h, h + 4, 0) for h in range(4)] + [(h, h + 4, 1) for h in range(8, 12)]


def build_program(NL=4, dbg_mix=False, stop=None):
    import os
    stop = stop or os.environ.get('K_STOP')
    nc = bass.Bass("TRN2", target_bir_lowering=False)
    P = Prog(nc)
    kinds = [i % 3 for i in range(NL)]
    def din(name, shape):
        return nc.dram_tensor(name, shape, F32, kind="ExternalInput").ap()

    xT = din("xT", [D, S])
    outT = nc.dram_tensor("outT", [D, S], F32, kind="ExternalOutput").ap()
    w_in, w_out, lng, lnb, sink = [], [], [], [], []
    for i in range(NL):
        ncols = {0: 2304, 1: 5632, 2: 2560}[kinds[i]]
        w_in.append(din("w_in_%d" % i, [D, ncols]))
        w_out.append(din("w_out_%d" % i, [D, D]))
        lng.append(din("lng_%d" % i, [128, 8]))
        lnb.append(din("lnb_%d" % i, [128, 8]))
        sink.append(din("sink_%d" % i, [1, 16]) if kinds[i] == 0 else None)
    ropeC_d = din("ropeC", [128, S])
    ropeS_d = din("ropeS", [128, S])
    perm_d = din("permT", [128, 128])
    mswa_d = din("mask_swa", [128, 512])
    mdil_d = din("mask_dil", [128, 512])
    mown_d = din("mask_own", [128, 4 * 512])
    erow_d = din("erows", [16, S])
    ident_d = din("ident", [128, 128])

    def dscr(name, shape, dt=BF16):
        return nc.dram_tensor(name, shape, dt, kind="Internal").ap()

    qs = dscr("qs", [3, D, S])
    ks = dscr("ks", [3, 256, S])
    vs = dscr("vs", [3, S, 256])
    zs = dscr("zs", [D, S])
    ys = dscr("ys", [D, S])
    q32 = dscr("q32", [D, S], F32)
    ksum_d = dscr("ksum_d", [256, 16], F32)

    XH = nc.alloc_sbuf_tensor("XH", [128, 8, S], BF16)
    XL = nc.alloc_sbuf_tensor("XL", [128, 8, S], BF16)
    ACOLS = 40448
    arena_t = nc.alloc_sbuf_tensor("arena", [128, ACOLS], BF16)
    A = Arena(arena_t, ACOLS)
    ps = [nc.alloc_psum_tensor("ps%d" % i, [128, 512], F32)[:, :] for i in range(8)]

    def xres(j, t):
        return "X_%d_%d" % (j, t)

    A.reset()
    stg = [A.alloc(512, F32) for _ in range(3)]
    n = 0
    for t in range(NT):
        for j in range(8):
            b = n % 3
            n += 1
            P.dma("sp", stg[b], xT[128 * j:128 * j + 128, T * t:T * t + T], [], ["stg%d" % b])
            P.cp("act", XH[:, j, T * t:T * t + T], stg[b], ["stg%d" % b], [xres(j, t) + "h"])
            P.tt("dve", XL[:, j, T * t:T * t + T], stg[b], XH[:, j, T * t:T * t + T],
                 ALU.subtract, ["stg%d" % b, xres(j, t) + "h"], [xres(j, t) + "l"])
    P.barrier()
    if stop == 'load':
        return _finish(nc, P)

    for li in range(NL):
        kind = kinds[li]
        groups, zbase, nkv = layer_cols(kind)
        pairs = head_pairs(kind)
        nkvp = nkv // 2
        ng = len(groups)
        wv = w_in[li].rearrange("(kc p) n -> p kc n", p=128)

        _skip = os.environ.get('K_SKIP_PA') == '1'
        A.reset()
        ropeC = A.alloc(S, F32)
        ropeS = A.alloc(S, F32)
        Wb = [A.alloc(1024) for _ in range(3)]
        permT = A.alloc(128)
        qb = [A.alloc(512) for _ in range(2)]
        t1 = [A.alloc(512, F32) for _ in range(2)]
        t2 = [A.alloc(512, F32) for _ in range(2)]
        qr = [A.alloc(512, F32) for _ in range(2)]
        stb = [A.alloc(512) for _ in range(3)]
        ksum = A.alloc(32, F32)
        P.dma("sp", ropeC, ropeC_d, [], ["ropeC"])
        P.dma("sp", ropeS, ropeS_d, [], ["ropeS"])
        P.dma("pool", permT, perm_d, [], ["permT"])

        chunks = []
        for g, (qbs, kbs, vbs) in enumerate(groups):
            for kp in range(nkvp):
                chunks.append(("K", g, kp, [(kbs + 128 * kp, 128, 0)]))
                chunks.append(("V", g, kp, [(vbs + 128 * kp, 128, 0)]))
        for pi, (ha, hb, kp) in enumerate(pairs):
            for g, (qbs, kbs, vbs) in enumerate(groups):
                chunks.append(("Q", g, pi, [(qbs + 64 * ha, 64, 0), (qbs + 64 * hb, 64, 64)]))
            chunks.append(("Z", 0, pi, [(zbase + 64 * ha, 64, 0), (zbase + 64 * hb, 64, 64)]))

        cnt = {"w": 0, "ps": 0, "pp": 0, "qb": 0, "t": 0, "st": 0}
        _kt = os.environ.get('K_TYPES')
        if _kt:
            chunks = [c for c in chunks if c[0] in _kt][:int(os.environ.get('K_NCH', '100'))]
        pending = []

        def flush_pending():
            while pending:
                pending.pop(0)()

        if _skip:
            chunks = []
        for (typ, g, idx, srcs) in chunks:
            wi = cnt["w"] % 3
            cnt["w"] += 1
            W = Wb[wi].rearrange("p (kc n) -> p kc n", kc=8)
            for (c0, ncs, d0) in srcs:
                P.dma("pool", W[:, :, d0:d0 + ncs], wv[:, :, c0:c0 + ncs], [], ["W%d" % wi])
            if typ == "V":
                for s4 in range(8):
                    pb = cnt["ps"] % 3
                    cnt["ps"] += 1
                    for ss in range(4):
                        s = 4 * s4 + ss
                        for kc in range(8):
                            P.mm(ps[pb][:, 128 * ss:128 * ss + 128], XH[:, kc, 128 * s:128 * s + 128],
                                 W[:, kc, :], kc == 0, kc == 7,
                                 ["W%d" % wi, xres(kc, s // 4) + "h"], ["ps%d" % pb])
                    sb = cnt["st"] % 3
                    cnt["st"] += 1
                    P.cp("act", stb[sb], ps[pb], ["ps%d" % pb], ["stb%d" % sb])
                    dst = vs[g].rearrange("(s p) c -> p s c", p=128)[:, 4 * s4:4 * s4 + 4, 128 * idx:128 * idx + 128]
                    P.dma("sp", dst, stb[sb].rearrange("p (s c) -> p s c", s=4), ["stb%d" % sb], ["vs"])
                continue
            for t in range(NT):
                pb = cnt["ps"] % 3
                cnt["ps"] += 1
                for kc in range(8):
                    P.mm(ps[pb], W[:, kc, :], XH[:, kc, T * t:T * t + T], kc == 0, kc == 7,
                         ["W%d" % wi, xres(kc, t) + "h"], ["ps%d" % pb])
                flush_pending()
                if typ == "Z":
                    sb = cnt["st"] % 3
                    cnt["st"] += 1
                    P.act(stb[sb], ps[pb], AF.Silu, ["ps%d" % pb], ["stb%d" % sb])
                    P.dma("sp", zs[128 * idx:128 * idx + 128, T * t:T * t + T], stb[sb],
                          ["stb%d" % sb], ["zs"])
                    continue
                qi = cnt["qb"] % 2
                cnt["qb"] += 1
                P.cp("act", qb[qi], ps[pb], ["ps%d" % pb], ["qb%d" % qi])
                _lvl = int(os.environ.get('K_LVL', '9'))
                if _lvl == 1:
                    P.dma("sp", ks[g, 128 * idx:128 * idx + 128, T * t:T * t + T], qb[qi], ["qb%d" % qi], ["ks"])
                    continue
                P.tt("dve", t1[qi], ps[pb], ropeC[:, T * t:T * t + T], ALU.mult,
                     ["ps%d" % pb, "ropeC"], ["t1_%d" % qi])
                if _lvl == 2:
                    continue
                if _lvl == 3:
                    pp = 3 + cnt["pp"] % 2
                    cnt["pp"] += 1
                    P.mm(ps[pp], permT, qb[qi], True, True, ["permT", "qb%d" % qi], ["ps%d" % pp])
                    continue

                def second(qi=qi, typ=typ, g=g, idx=idx, t=t):
                    pp = 3 + cnt["pp"] % 2
                    cnt["pp"] += 1
                    P.mm(ps[pp], permT, qb[qi], True, True, ["permT", "qb%d" % qi], ["ps%d" % pp])
                    P.tt("dve", t2[qi], ps[pp], ropeS[:, T * t:T * t + T], ALU.mult,
                         ["ps%d" % pp, "ropeS"], ["t2_%d" % qi])
                    sb = cnt["st"] % 3
                    cnt["st"] += 1
                    need32 = (kind == 2)
                    if need32:
                        P.tt("dve", qr[qi], t1[qi], t2[qi], ALU.add,
                             ["t1_%d" % qi, "t2_%d" % qi], ["qr%d" % qi])
                        P.cp("act", stb[sb], qr[qi], ["qr%d" % qi], ["stb%d" % sb])
                    else:
                        P.tt("dve", stb[sb], t1[qi], t2[qi], ALU.add,
                             ["t1_%d" % qi, "t2_%d" % qi], ["stb%d" % sb])
                    if typ == "Q":
                        P.dma("sp", qs[g, 128 * idx:128 * idx + 128, T * t:T * t + T], stb[sb],
                              ["stb%d" % sb], ["qs"])
                        if need32:
                            P.dma("sp", q32[128 * idx:128 * idx + 128, T * t:T * t + T], qr[qi],
                                  ["qr%d" % qi], ["q32"])
                    else:
                        P.dma("sp", ks[g, 128 * idx:128 * idx + 128, T * t:T * t + T], stb[sb],
                              ["stb%d" % sb], ["ks"])
                        if need32:
                            P.reduce_add("dve", ksum[:, 2 * t:2 * t + 2],
                                         qr[qi].rearrange("p (b k) -> p b k", b=2),
                                         ["qr%d" % qi], ["ksum"])

                pending.append(second)
            flush_pending()
            if typ == "K" and kind == 2:
                P.dma("sp", ksum_d[128 * idx:128 * idx + 128, :], ksum[:, 0:16], ["ksum"], ["ksum_d"])
        P.barrier()
        if stop and stop[0] == 'P' and li == NL - 1:
            which = stop[1:]
            src = {'q0': qs[0], 'q1': qs[1], 'q2': qs[2], 'z': zs}.get(which)
            if which == 'XH':
                for c in range(8):
                    P.dma("pool", outT[128 * c:128 * c + 128, :], XH[:, c, :], [], ["outT"])
            elif src is not None:
                for c in range(8):
                    P.dma("pool", outT[128 * c:128 * c + 128, :], src[128 * c:128 * c + 128, :], [], ["outT"])
            elif which[0] == 'k':
                g = int(which[1])
                for c in range(2):
                    P.dma("pool", outT[128 * c:128 * c + 128, :], ks[g, 128 * c:128 * c + 128, :], [], ["outT"])
            elif which[0] == 'v':
                g = int(which[1])
                for c in range(8):
                    P.dma("pool", outT[0:256, 512 * c:512 * c + 512].rearrange("c s -> s c"),
                          vs[g, 512 * c:512 * c + 512, :], [], ["outT"])
            return _finish(nc, P)

        A.reset()
        if os.environ.get('K_PAD'):
            A.alloc(int(os.environ['K_PAD']))
        yst = [A.alloc(512) for _ in range(2)]
        SZ = [A.alloc(512) for _ in range(3)]
        loads_list = []
        _tight = kind == 1 or os.environ.get('K_TIGHT') == '1'
        NPT = 3 if _tight else 4
        NQB = 2 if _tight else 3
        PT = [A.alloc(512) for _ in range(NPT)]
        rd = A.alloc(512, F32)
        gg = rd
        initrow = A.alloc(16 * 128 if kind == 0 else 128)
        onesrow = A.alloc(512)
        P.memset("dve", initrow[0:1, :], 0.0, ["initrow"])
        P.memset("dve", onesrow[0:1, :], 1.0, ["onesrow"])
        if kind == 0:
            sk = A.alloc(16, F32)
            P.dma("sp", sk[0:1, :], sink[li], [], ["sk"])
            for h in range(16):
                isB = h >= 8
                c0 = 128 * h + (0 if isB else 64)
                P.act(initrow[0:1, c0:c0 + 64], sk[0:1, h:h + 1].to_broadcast([1, 64]), AF.Exp,
                      ["sk", "initrow"], ["initrow"])

        VW = 32 * 192

        def load_v(g, kp, Vt, dil, tag):
            V3 = Vt.rearrange("p (b c) -> p b c", c=192)
            P.memset("pool", V3[:, :, 64:128], 1.0, ["Vt" + tag])
            nb = 32 // dil
            for r in range(dil):
                for half in range(2):
                    c0 = 128 * kp + 64 * half
                    src = vs[g].rearrange("(b p r) c -> r p b c", p=128, r=dil)[r][:, :, c0:c0 + 64]
                    dst = V3[:, nb * r:nb * (r + 1), 128 * half:128 * half + 64]
                    P.dma("sp", dst, src, ["vs"], ["Vt" + tag])

        def load_kv(g, kp, KT, Vt, dil, tag):
            P.dma("sp", KT, ks[g, 128 * kp:128 * kp + 128, :], ["ks"], ["KT" + tag])
            load_v(g, kp, Vt, dil, tag)

        def vaug(Vt, kt, isB):
            base = 192 * kt + (64 if isB else 0)
            return Vt[:, base:base + 128]

        items = []
        st = {"s": 0, "pt": 0, "o": 0, "ld": 0}

        def epilogue(pi, t, oA, oB, szb):
            P.recip("dve", rd[0:64, :], ps[oA][64:128, :], ["ps%d" % oA], ["gg"])
            P.recip("dve", rd[64:128, :], ps[oB][0:64, :], ["ps%d" % oB], ["gg"])
            P.tt("pool", gg, rd, SZ[szb], ALU.mult, ["SZ%d" % szb, "gg"], ["gg"])
            yb = st["ld"] % 2
            if os.environ.get('K_DEN') == '1':
                P.cp("dve", yst[yb][0:64, :], ps[oA][64:128, :], ["ps%d" % oA], ["yst%d" % yb])
                P.cp("dve", yst[yb][64:128, :], ps[oB][0:64, :], ["ps%d" % oB], ["yst%d" % yb])
                P.dma("sp", ys[128 * pi:128 * pi + 128, T * t:T * t + T], yst[yb], ["yst%d" % yb], ["ys"])
                st["ld"] += 1
                return
            P.tt("dve", yst[yb][0:64, :], ps[oA][0:64, :], gg[0:64, :], ALU.mult,
                 ["ps%d" % oA, "gg"], ["yst%d" % yb])
            P.tt("dve", yst[yb][64:128, :], ps[oB][64:128, :], gg[64:128, :], ALU.mult,
                 ["ps%d" % oB, "gg"], ["yst%d" % yb])
            P.dma("sp", ys[128 * pi:128 * pi + 128, T * t:T * t + T], yst[yb], ["yst%d" % yb], ["ys"])
            st["ld"] += 1

        if kind in (0, 1):
            mask4 = A.alloc(512)
            P.dma("pool", mask4, mswa_d if kind == 0 else mdil_d, [], ["mask4"])
            KTs = [A.alloc(S) for _ in range(ng)]
            Vts = [A.alloc(VW) for _ in range(ng)]
            QTs = [[A.alloc(512) for _ in range(NQB)] for _ in range(ng)]
            cur_kp = -1
            for pi, (ha, hb, kp) in enumerate(pairs):
                newkp = kp != cur_kp
                cur_kp = kp

                def kvload(kp=kp):
                    for g in range(ng):
                        if kind == 1 and str(g) not in os.environ.get('K_GRP', '012') + os.environ.get('K_LDG', ''):
                            continue
                        load_kv(g, kp, KTs[g], Vts[g], DIL[g][1] if kind == 1 else 1, str(g))
                for t in range(NT):
                    ti = pi * NT + t
                    lb = ti % 3
                    lq = ti % NQB
                    flush_before = newkp and t == 0

                    def loads(pi=pi, t=t, lb=lb, lq=lq):
                        for g in range(ng):
                            if kind == 1 and str(g) not in os.environ.get('K_GRP', '012') + os.environ.get('K_LDQ', ''):
                                continue
                            P.dma("sp", QTs[g][lq], qs[g, 128 * pi:128 * pi + 128, T * t:T * t + T],
                                  ["qs"], ["QT%d_%d" % (g, lq)])
                        P.dma("sp", SZ[lb], zs[128 * pi:128 * pi + 128, T * t:T * t + T],
                              ["zs"], ["SZ%d" % lb])

                    loads_list.append(loads)
                    obanks = []
                    first_item_of_tile = True
                    for xi, h in enumerate((ha, hb)):
                        isB = xi == 1
                        R = slice(64, 128) if isB else slice(0, 64)
                        ob = 4 + (st["o"] % 4)
                        st["o"] += 1
                        obanks.append(ob)
                        banks = []
                        for g in range(ng):
                            if kind == 1 and str(g) not in os.environ.get('K_GRP', '012'):
                                continue
                            dil = DIL[g][1] if kind == 1 else 1
                            nq = T // dil
                            slots = []
                            if nq >= 128:
                                nb = 32 // dil
                                for r in range(dil):
                                    for bq in range(nq // 128):
                                        fb = (T * t // dil) // 128 + bq
                                        qsl = slice(r + dil * 128 * bq, r + dil * 128 * bq + dil * 127 + 1, dil) if dil > 1 \
                                            else slice(128 * bq, 128 * bq + 128)
                                        for which, kb in ((0, fb - 1), (1, fb)):
                                            valid = kb >= 0
                                            kbb = kb if valid else fb
                                            ksl = slice(r + dil * 128 * kbb, r + dil * 128 * kbb + dil * 127 + 1, dil) if dil > 1 \
                                                else slice(128 * kbb, 128 * kbb + 128)
                                            slots.append((g, ksl, qsl, r * nb + kbb, valid, which, 128, 0))
                            else:
                                bb = (T * t) // 2048
                                u = ((T * t) % 2048) // T
                                nb = 2
                                for r in range(dil):
                                    qsl = slice(r, r + dil * (nq - 1) + 1, dil)
                                    for which, kb in ((0, bb - 1), (1, bb)):
                                        valid = kb >= 0
                                        kbb = kb if valid else bb
                                        ksl = slice(r + 2048 * kbb, r + 2048 * kbb + dil * 127 + 1, dil)
                                        slots.append((g, ksl, qsl, r * nb + kbb, valid, which, nq, nq * u))
                            cur, used = [], 0
                            for sl in slots:
                                if used + sl[6] > 512:
                                    banks.append(cur)
                                    cur, used = [], 0
                                cur.append(sl + (used,))
                                used += sl[6]
                            if cur:
                                banks.append(cur)
                        for bi, bank in enumerate(banks):
                            def qk(bank=bank, R=R, lb=lq, first=(bi == 0), ob=ob, h=h, do_loads=first_item_of_tile,
                                   ti=ti, kvl=(kvload if (flush_before and first_item_of_tile) else None),
                                   _pi=pi, _xi=xi, _t=t):
                                if kvl is not None:
                                    kvl()
                                if do_loads:
                                    if ti == 0:
                                        loads_list[0]()
                                    if ti + 1 < len(loads_list):
                                        loads_list[ti + 1]()
                                if os.environ.get('K_PTD') == '1' and _pi == 0 and _xi == 0 and first:
                                    P.dma("pool", q32[256:384, T * _t:T * _t + T], QTs[0][lb],
                                          ["QT0_%d" % lb], ["q32"])
                                    if _t == 0:
                                        P.dma("pool", q32[384:512, :], KTs[0], ["KT0"], ["q32"])
                                sbk = st["s"] % 3
                                st["s"] += 1
                                if first:
                                    ih = h if kind == 0 else 0
                                    P.mm(ps[ob], initrow[0:1, 128 * ih:128 * ih + 128], onesrow[0:1, :], True, False,
                                         ["initrow", "onesrow"], ["ps%d" % ob])
                                for (g, ksl, qsl, kt, valid, which, w, moff, col) in bank:
                                    P.mm(ps[sbk][:, col:col + w], KTs[g][R, ksl], QTs[g][lb][R, qsl], True, True,
                                         ["KT%d" % g, "QT%d_%d" % (g, lb)], ["ps%d" % sbk])
                                return sbk

                            def sm(sbk, bank=bank, _pi=pi, _xi=xi, _bi=bi, _t=t):
                                pt = st["pt"] % NPT
                                st["pt"] += 1
                                ncol = sum(sl[6] for sl in bank)
                                P.act(PT[pt][:, 0:ncol], ps[sbk][:, 0:ncol], AF.Exp, ["ps%d" % sbk], ["PT%d" % pt],
                                      scale=SCALE)
                                w = bank[0][6]
                                if w == 128:
                                    P.tt("dve", PT[pt][:, 0:ncol], PT[pt][:, 0:ncol], mask4[:, 0:ncol], ALU.mult,
                                         ["PT%d" % pt, "mask4"], ["PT%d" % pt])
                                else:
                                    moff = bank[0][7]
                                    nsl = len(bank) // 2
                                    mv = mask4[:, 0:256].rearrange("p (a c) -> p a c", a=2)[:, :, moff:moff + w]
                                    for half in range(0, nsl, 1):
                                        pass
                                    pv_ = PT[pt][:, 0:ncol].rearrange("p (s a c) -> p s a c", a=2, c=w)
                                    P.op("dve", lambda e, pv_=pv_, mv=mv, nsl=nsl, w=w: e.tensor_tensor(
                                        pv_, pv_, mv.unsqueeze(1).to_broadcast([128, nsl, 2, w]), ALU.mult),
                                        ["PT%d" % pt, "mask4"], ["PT%d" % pt])
                                if os.environ.get('K_PTD') == '1' and _pi == 0 and _xi == 0:
                                    P.dma("pool", q32[128 * _bi:128 * _bi + 128, T * _t:T * _t + T], PT[pt],
                                          ["PT%d" % pt], ["q32"])
                                return pt

                            def pv(pt, bank=bank, isB=isB, ob=ob, last=(bi == len(banks) - 1)):
                                vb = [sl for sl in bank if sl[4]]
                                for i, (g, ksl, qsl, kt, valid, which, w, moff, col) in enumerate(vb):
                                    P.mm(ps[ob][:, qsl], vaug(Vts[g], kt, isB), PT[pt][:, col:col + w], False,
                                         last and i == len(vb) - 1,
                                         ["Vt%d" % g, "PT%d" % pt], ["ps%d" % ob])

                            items.append([qk, sm, pv, None, flush_before and first_item_of_tile])
                            first_item_of_tile = False
                    oA, oB = obanks
                    items[-1][3] = (lambda pi=pi, t=t, oA=oA, oB=oB, lb=lb: epilogue(pi, t, oA, oB, lb))
        else:
            mown = A.alloc(4 * 512)
            P.dma("pool", mown, mown_d, [], ["mown"])
            ident = A.alloc(128)
            P.dma("pool", ident, ident_d, [], ["ident"])
            KX = [A.alloc(S) for _ in range(2)]
            Vt = A.alloc(VW)
            QX = [[A.alloc(512) for _ in range(3)] for _ in range(2)]
            Q32 = [[A.alloc(512, F32) for _ in range(3)] for _ in range(2)]
            ksA = [A.alloc(16, F32) for _ in range(2)]
            gsb = A.alloc(64, F32)
            top8 = A.alloc(32, F32)
            selb = A.alloc(64, F32)
            biasp = A.alloc(4 * 80)
            P.memset("dve", biasp, 0.0, ["biasp"])
            psT = ps[3].bitcast(BF16)
            cur_kp = -1
            for pi, (ha, hb, kp) in enumerate(pairs):
                newkp = kp != cur_kp
                cur_kp = kp

                def kvload(kp=kp):
                    load_v(0, kp, Vt, 1, "m")
                    for xi in range(2):
                        P.dma("sp", KX[xi][0:64, :], ks[0, 128 * kp + 64 * xi:128 * kp + 64 * xi + 64, :],
                              ["ks"], ["KX%d" % xi])
                        P.dma("pool", KX[xi][64:80, :], erow_d, [], ["KX%d" % xi])
                        P.dma("sp", ksA[xi][0:64, :], ksum_d[128 * kp + 64 * xi:128 * kp + 64 * xi + 64, :],
                              ["ksum_d"], ["ksA%d" % xi])
                for t in range(NT):
                    ti = pi * NT + t
                    lb = ti % 3
                    flush_before = newkp and t == 0

                    def loads(pi=pi, t=t, lb=lb):
                        for xi in range(2):
                            r0 = 128 * pi + 64 * xi
                            P.dma("sp", QX[xi][lb][0:64, :], qs[0, r0:r0 + 64, T * t:T * t + T],
                                  ["qs"], ["QX%d_%d" % (xi, lb)])
                            P.dma("sp", Q32[xi][lb][0:64, :], q32[r0:r0 + 64, T * t:T * t + T],
                                  ["q32"], ["Q32%d_%d" % (xi, lb)])
                        P.dma("sp", SZ[lb], zs[128 * pi:128 * pi + 128, T * t:T * t + T],
                              ["zs"], ["SZ%d" % lb])

                    loads_list.append(loads)

                    def gate(xi, t=t, lb=lb):
                        gp = 2
                        for j in range(4):
                            P.mm(ps[gp][:, 16 * j:16 * j + 16], Q32[xi][lb][0:64, 128 * j:128 * j + 128],
                                 ksA[xi][0:64, :], True, True,
                                 ["Q32%d_%d" % (xi, lb), "ksA%d" % xi], ["ps%d" % gp])
                        g3 = gsb.rearrange("p (j n) -> p j n", j=4)
                        s3 = selb.rearrange("p (j n) -> p j n", j=4)
                        t3 = top8.rearrange("p (j n) -> p j n", j=4)
                        b3 = biasp.rearrange("p (j n) -> p j n", j=4)
                        P.memset("dve", gsb, -1e30, ["gsb"])
                        for j in range(4):
                            own = 2 * t + j // 2
                            if own > 0:
                                P.cp("dve", g3[:, j, 0:own], ps[gp][:, 16 * j:16 * j + own], ["ps%d" % gp], ["gsb"])
                        P.memset("dve", selb, -BIG, ["selb"])
                        for j in range(4):
                            own = 2 * t + j // 2
                            if own > 3:
                                P.max8("dve", t3[:, j, :], g3[:, j, :], ["gsb"], ["top8"])
                                P.ts("dve", s3[:, j, 0:own], g3[:, j, 0:own], t3[:, j, 2:3], None, ALU.is_ge, None,
                                     ["gsb", "top8", "selb"], ["selb"])
                                P.ts("dve", s3[:, j, 0:own], s3[:, j, 0:own], BIG, -BIG, ALU.mult, ALU.add,
                                     ["selb"], ["selb"])
                                P.memset("dve", s3[:, j, own:own + 1], 0.0, ["selb"])
                            else:
                                P.memset("dve", s3[:, j, 0:own + 1], 0.0, ["selb"])
                        P.cp("dve", b3[:, :, 64:80], s3, ["selb", "biasp"], ["biasp"])
                        for j in range(4):
                            P.tr(psT[0:80, 128 * j:128 * j + 128], b3[:, j, :], ident, ["biasp", "ident"], ["ps3"])
                        P.cp("dve", QX[xi][lb][64:80, :], psT[64:80, 0:512], ["ps3"], ["QX%d_%d" % (xi, lb)])

                    obanks = []
                    first_item_of_tile = True
                    for xi, h in enumerate((ha, hb)):
                        isB = xi == 1
                        ob = 4 + (st["o"] % 4)
                        st["o"] += 1
                        obanks.append(ob)
                        nkt = 4 * (t + 1)
                        for kt in range(nkt):
                            def qk(kt=kt, xi=xi, lb=lb, ob=ob, h=h, do_loads=first_item_of_tile, gate=gate,
                                   ti=ti, kvl=(kvload if (flush_before and first_item_of_tile) else None)):
                                if kvl is not None:
                                    kvl()
                                if do_loads:
                                    if ti == 0:
                                        loads_list[0]()
                                    if ti + 1 < len(loads_list):
                                        loads_list[ti + 1]()
                                if kt == 0:
                                    gate(xi)
                                    P.mm(ps[ob], initrow[0:1, 0:128], onesrow[0:1, :], True, False,
                                         ["initrow", "onesrow"], ["ps%d" % ob])
                                sbk = st["s"] % 2
                                st["s"] += 1
                                P.mm(ps[sbk], KX[xi][0:80, 128 * kt:128 * kt + 128], QX[xi][lb][0:80, :], True, True,
                                     ["KX%d" % xi, "QX%d_%d" % (xi, lb)], ["ps%d" % sbk])
                                return sbk

                            def sm(sbk, kt=kt, t=t):
                                pt = st["pt"] % 4
                                st["pt"] += 1
                                P.act(PT[pt], ps[sbk], AF.Exp, ["ps%d" % sbk], ["PT%d" % pt], scale=SCALE)
                                if kt >= 4 * t:
                                    m = mown[:, 512 * (kt - 4 * t):512 * (kt - 4 * t) + 512]
                                    P.tt("pool", PT[pt], PT[pt], m, ALU.mult, ["PT%d" % pt, "mown"], ["PT%d" % pt])
                                return pt

                            def pv(pt, kt=kt, isB=isB, ob=ob, last=(kt == nkt - 1)):
                                P.mm(ps[ob], vaug(Vt, kt, isB), PT[pt], False, last,
                                     ["Vtm", "PT%d" % pt], ["ps%d" % ob])

                            items.append([qk, sm, pv, None, flush_before and first_item_of_tile])
                            first_item_of_tile = False
                    oA, oB = obanks
                    items[-1][3] = (lambda pi=pi, t=t, oA=oA, oB=oB, lb=lb: epilogue(pi, t, oA, oB, lb))

        if _skip:
            items = []
        prev = None
        for it in items:
            if it[4] and prev is not None:
                prev[0][2](prev[1])
                if prev[0][3] is not None:
                    prev[0][3]()
                prev = None
            sbk = it[0]()
            if prev is not None:
                prev[0][2](prev[1])
                if prev[0][3] is not None:
                    prev[0][3]()
            pt = it[1](sbk)
            prev = (it, pt)
        if prev is not None:
            prev[0][2](prev[1])
            if prev[0][3] is not None:
                prev[0][3]()
        P.barrier()
        if stop == 'A' and li == NL - 1:
            for c in range(8):
                if os.environ.get('K_XHD') == '1':
                    P.dma("pool", outT[128 * c:128 * c + 128, :], XH[:, c, :], [], ["outT"])
                elif os.environ.get('K_PTD') == '1':
                    P.dma("sp", outT[128 * c:128 * c + 128, :], q32[128 * c:128 * c + 128, :], [], ["outT"])
                else:
                    P.dma("pool", outT[128 * c:128 * c + 128, :], ys[128 * c:128 * c + 128, :], [], ["outT"])
            return _finish(nc, P)

        A.reset()
        Yb = [A.alloc(8 * 512) for _ in range(2)]
        Wo = A.alloc(8 * 1024)
        T32 = [A.alloc(8 * 512, F32) for _ in range(2)]
        SQ = [A.alloc(512, F32) for _ in range(2)]
        mean_sb = A.alloc(512, F32)
        rstd_sb = A.alloc(512, F32)
        m2 = A.alloc(512, F32)
        onesM = A.alloc(128, F32)
        G = A.alloc(8, F32)
        Bt = A.alloc(8, F32)
        P.memset("dve", onesM, 1.0 / D, ["onesM"])
        P.dma("sp", G, lng[li], [], ["G"])
        P.dma("sp", Bt, lnb[li], [], ["Bt"])
        Wo3 = Wo.rearrange("p (c n) -> p c n", c=8)
        for pi, (ha, hb, kp) in enumerate(pairs):
            P.dma("pool", Wo3[0:64, pi, :], w_out[li][64 * ha:64 * ha + 64, :], [], ["Wo"])
            P.dma("pool", Wo3[64:128, pi, :], w_out[li][64 * hb:64 * hb + 64, :], [], ["Wo"])
        last_layer = li == NL - 1
        for t in range(NT):
            yb = t % 2
            Y3 = Yb[yb].rearrange("p (c n) -> p c n", c=8)
            P.dma("sp", Y3, ys.rearrange("(c p) s -> p c s", p=128)[:, :, T * t:T * t + T], ["ys"], ["Y%d" % yb])
            T3 = T32[yb].rearrange("p (c n) -> p c n", c=8)
            tr_ = "T32_%d" % yb
            for j in range(8):
                pb = j % 3
                for c in range(8):
                    P.mm(ps[pb], Wo3[:, c, 128 * j:128 * j + 128], Y3[:, c, :], c == 0, c == 7,
                         ["Wo", "Y%d" % yb], ["ps%d" % pb])
                P.stt("dve", T3[:, j, :], XH[:, j, T * t:T * t + T], ALPHA, ps[pb], ALU.mult, ALU.add,
                      ["ps%d" % pb, xres(j, t) + "h"], [tr_ + "_%d" % j])
                P.stt("dve", T3[:, j, :], XL[:, j, T * t:T * t + T], ALPHA, T3[:, j, :], ALU.mult, ALU.add,
                      [tr_ + "_%d" % j, xres(j, t) + "l"], [tr_ + "_%d" % j])
            _ol = int(os.environ.get('K_OLVL', '9'))
            if _ol == 1:
                continue
            for j in range(8):
                P.mm(ps[3], onesM, T3[:, j, :], j == 0, j == 7, ["onesM", tr_ + "_%d" % j], ["ps3"])
            for j in range(8):
                sq = j % 2
                P.act(SQ[sq], T3[:, j, :], AF.Square, [tr_ + "_%d" % j], ["SQ%d" % sq])
                P.mm(ps[4], onesM, SQ[sq], j == 0, j == 7, ["onesM", "SQ%d" % sq], ["ps4"])
            if _ol == 2:
                continue
            P.cp("act", mean_sb, ps[3], ["ps3"], ["mean"])
            P.tt("dve", m2, mean_sb, mean_sb, ALU.mult, ["mean"], ["m2"])
            P.tt("dve", m2, ps[4], m2, ALU.subtract, ["ps4", "m2"], ["m2"])
            P.ts("dve", m2, m2, EPS, None, ALU.add, None, ["m2"], ["m2"])
            P.act(rstd_sb, m2, AF.Sqrt, ["m2"], ["rstd"])
            P.recip("dve", rstd_sb, rstd_sb, ["rstd"], ["rstd"])
            if _ol == 3:
                continue
            for j in range(8):
                rn = tr_ + "_%d" % j
                P.tt("dve", T3[:, j, :], T3[:, j, :], mean_sb, ALU.subtract, [rn, "mean"], [rn])
                P.tt("dve", T3[:, j, :], T3[:, j, :], rstd_sb, ALU.mult, [rn, "rstd"], [rn])
                P.ts("dve", T3[:, j, :], T3[:, j, :], G[:, j:j + 1], Bt[:, j:j + 1], ALU.mult, ALU.add,
                     [rn, "G", "Bt"], [rn])
                if last_layer:
                    P.dma("sp", outT[128 * j:128 * j + 128, T * t:T * t + T], T3[:, j, :], [rn], ["outT"])
                else:
                    P.cp("act", XH[:, j, T * t:T * t + T], T3[:, j, :], [rn], [xres(j, t) + "h"])
                    P.tt("dve", XL[:, j, T * t:T * t + T], T3[:, j, :], XH[:, j, T * t:T * t + T], ALU.subtract,
                         [rn, xres(j, t) + "h"], [xres(j, t) + "l"])
        P.barrier()
        if stop == 'O%d' % li:
            for j in range(8):
                src = XL if os.environ.get('K_XL') == '1' else XH
                P.dma("pool", outT[128 * j:128 * j + 128, :], src[:, j, :], [], ["outT"])
            return _finish(nc, P)

    return _finish(nc, P)


def _finish(nc, P):
    with nc.allow_low_precision("bf16 matmul operands, fp32 accumulation"):
        with nc.allow_non_contiguous_dma("layout"):
            P.emit()
    return nc


def _consts():
    pos = np.arange(S, dtype=np.float32)
    inv = (np.float32(500000.0) ** (-np.arange(0, 16, 2, dtype=np.float32) / np.float32(16))).astype(np.float32)
    ang = pos[None, :] * inv[:, None]
    cos = np.cos(ang).astype(np.float32)
    sin = np.sin(ang).astype(np.float32)
    C = np.ones((128, S), np.float32)
    Sn = np.zeros((128, S), np.float32)
    permT = np.zeros((128, 128), np.float32)
    for base in (0, 64):
        C[base:base + 8] = cos
        C[base + 8:base + 16] = cos
        Sn[base:base + 8] = sin
        Sn[base + 8:base + 16] = sin
        for i in range(8):
            permT[base + i + 8, base + i] = -1.0
            permT[base + i, base + i + 8] = 1.0
    j = np.arange(128)[:, None]
    r = np.arange(128)[None, :]
    cur = (j <= r).astype(np.float32)
    prev_swa = (j >= r + 1).astype(np.float32)
    prev_dil = (j >= r).astype(np.float32)
    mswa = np.concatenate([prev_swa, cur, prev_swa, cur], axis=1)
    mdil = np.concatenate([prev_dil, cur, prev_dil, cur], axis=1)
    one = np.ones((128, 128), np.float32)
    zero = np.zeros((128, 128), np.float32)
    mown = np.concatenate([
        cur, one, one, one,
        zero, cur, one, one,
        zero, zero, cur, one,
        zero, zero, zero, cur], axis=1)
    erows = np.zeros((16, S), np.float32)
    for n in range(16):
        erows[n, 256 * n:256 * n + 256] = 1.0
    ident = np.eye(128, dtype=np.float32)
    return dict(ropeC=C, ropeS=Sn, permT=permT, mask_swa=mswa, mask_dil=mdil, mask_own=mown,
                erows=erows, ident=ident)


_NC_CACHE = {}


def make_in_maps(inputs, NL, cores):
    consts = _consts()
    maps = []
    for c in cores:
        m = dict(consts)
        m["xT"] = np.ascontiguousarray(np.asarray(inputs["x"][c], dtype=np.float32).T)
        for i in range(NL):
            m["w_in_%d" % i] = np.ascontiguousarray(inputs["w_in_%d" % i], dtype=np.float32)
            m["w_out_%d" % i] = np.ascontiguousarray(inputs["w_out_%d" % i], dtype=np.float32)
            m["lng_%d" % i] = np.ascontiguousarray(np.asarray(inputs["ln_g_%d" % i], np.float32).reshape(8, 128).T)
            m["lnb_%d" % i] = np.ascontiguousarray(np.asarray(inputs["ln_b_%d" % i], np.float32).reshape(8, 128).T)
            if i % 3 == 0:
                m["sink_%d" % i] = np.asarray(inputs["sink_%d" % i], np.float32).reshape(1, 16)
        maps.append(m)
    return maps


def kernel(x, w_in_0, sink_0, w_out_0, ln_g_0, ln_b_0,
           w_in_1, w_out_1, ln_g_1, ln_b_1,
           w_in_2, w_out_2, ln_g_2, ln_b_2,
           w_in_3, sink_3, w_out_3, ln_g_3, ln_b_3):
    inputs = dict(x=x, w_in_0=w_in_0, sink_0=sink_0, w_out_0=w_out_0, ln_g_0=ln_g_0, ln_b_0=ln_b_0,
                  w_in_1=w_in_1, w_out_1=w_out_1, ln_g_1=ln_g_1, ln_b_1=ln_b_1,
                  w_in_2=w_in_2, w_out_2=w_out_2, ln_g_2=ln_g_2, ln_b_2=ln_b_2,
                  w_in_3=w_in_3, sink_3=sink_3, w_out_3=w_out_3, ln_g_3=ln_g_3, ln_b_3=ln_b_3)
    NL = 4
    if NL not in _NC_CACHE:
        _NC_CACHE[NL] = build_program(NL)
    nc = _NC_CACHE[NL]
    cores = list(range(8))
    in_maps = make_in_maps(inputs, NL, cores)
    res = run_bass_kernel_spmd(nc, in_maps, core_ids=cores)
    out = np.stack([np.asarray(r["outT"], dtype=np.float32).T for r in res.results], axis=0)
    return np.ascontiguousarray(out)
```
